# Optimizing a Trainium2 kernel written in Bass

```python
import math
import jax, jax.numpy as jnp
from jax import lax
import numpy as np

D_MODEL = 1024
BATCH = 4
SEQ = 4096
DEPTH = 2

PLE_DIM = 256
GRID_W = 64
EPS = 1e-6
ROPE_THETA = 10000.0

HA = 4
DH_A = 64
DV_A = 2 * DH_A
WIDTH_A = HA * DV_A
QBLK = 128

CB = 512
CONV_W = 31

HC = 8
DC = 64
WIDTH_C = HC * DC
NA_KR = 8
NA_KC = 16

D_FF = int(math.ceil(8 * D_MODEL / 3 / 256) * 256)

COLS = [
    HA * 2 * DH_A,
    HA * 2 * DH_A,
    WIDTH_A,
    2 * CB,
    WIDTH_C,
    WIDTH_C,
    WIDTH_C,
    3 * D_MODEL,
]
SPLITS = list(np.cumsum(COLS)[:-1].tolist())
D_IN = int(sum(COLS))

kernel_name = "hybrid_diffattn_conformer_natten_block"


def rmsnorm(x, g):
    x32 = x.astype(jnp.float32)
    y = x32 * lax.rsqrt(jnp.mean(x32 * x32, axis=-1, keepdims=True) + EPS)
    return (y * g.astype(jnp.float32)).astype(x.dtype)


def layernorm(x, g, b):
    x32 = x.astype(jnp.float32)
    mu = jnp.mean(x32, axis=-1, keepdims=True)
    var = jnp.mean(jnp.square(x32 - mu), axis=-1, keepdims=True)
    y = (x32 - mu) * lax.rsqrt(var + EPS)
    return (y * g.astype(jnp.float32) + b.astype(jnp.float32)).astype(x.dtype)


def rope_tables(seq, dim):
    inv = 1.0 / (ROPE_THETA ** (jnp.arange(0, dim, 2, dtype=jnp.float32) / dim))
    ang = jnp.arange(seq, dtype=jnp.float32)[:, None] * inv[None, :]
    ang = jnp.concatenate([ang, ang], axis=-1)
    return jnp.cos(ang), jnp.sin(ang)


def apply_rope(x, cos, sin):
    half = x.shape[-1] // 2
    x1, x2 = x[..., :half], x[..., half:]
    rot = jnp.concatenate([-x2, x1], axis=-1)
    return (x.astype(jnp.float32) * cos + rot.astype(jnp.float32) * sin).astype(x.dtype)


def diff_attention(qa, ka, va, lq1, lk1, lq2, lk2, subln_g, lam_init, cos, sin):
    B, S, _ = qa.shape
    q = qa.reshape(B, S, HA, 2, DH_A).transpose(0, 2, 3, 1, 4)
    k = ka.reshape(B, S, HA, 2, DH_A).transpose(0, 2, 3, 1, 4)
    v = va.reshape(B, S, HA, DV_A).transpose(0, 2, 1, 3)
    q = apply_rope(q, cos, sin)
    k = apply_rope(k, cos, sin)
    f32 = jnp.float32
    lam = (jnp.exp(jnp.sum(lq1.astype(f32) * lk1.astype(f32)))
           - jnp.exp(jnp.sum(lq2.astype(f32) * lk2.astype(f32))) + lam_init)
    scale = DH_A ** -0.5
    nb = S // QBLK
    qb = q.reshape(B, HA, 2, nb, QBLK, DH_A).transpose(3, 0, 1, 2, 4, 5)

    def block(qi):
        s = jnp.einsum('bhcqd,bhckd->bhcqk', qi, k).astype(f32) * scale
        a = jax.nn.softmax(s, axis=-1)
        a = (a[:, :, 0] - lam * a[:, :, 1]).astype(v.dtype)
        return jnp.einsum('bhqk,bhkd->bhqd', a, v)

    o = lax.map(block, qb)
    o = o.transpose(1, 2, 0, 3, 4).reshape(B, HA, S, DV_A)
    o = rmsnorm(o, subln_g) * (1.0 - lam_init)
    return o.transpose(0, 2, 1, 3).reshape(B, S, WIDTH_A)


def conformer_conv(u, conv_w, conv_b, cln_g, cln_b):
    a, g = jnp.split(u, 2, axis=-1)
    z = a * jax.nn.sigmoid(g)
    z = lax.conv_general_dilated(
        z, conv_w.astype(z.dtype), window_strides=(1,),
        padding=[(CONV_W // 2, CONV_W // 2)],
        dimension_numbers=('NWC', 'WIO', 'NWC'),
        feature_group_count=CB)
    z = z + conv_b
    z = layernorm(z, cln_g, cln_b)
    return jax.nn.silu(z)


def neighbourhood_attention(qc, kc, vc, rpb):
    B, S, _ = qc.shape
    R = S // GRID_W
    KR = min(NA_KR, R)
    q = qc.reshape(B, R, GRID_W, HC, DC).transpose(1, 0, 3, 2, 4)
    k = kc.reshape(B, R, GRID_W, HC, DC).transpose(0, 3, 1, 2, 4)
    v = vc.reshape(B, R, GRID_W, HC, DC).transpose(0, 3, 1, 2, 4)
    cols = jnp.arange(GRID_W)
    cs = jnp.clip(cols - NA_KC // 2, 0, GRID_W - NA_KC)
    col_idx = cs[:, None] + jnp.arange(NA_KC)[None, :]
    col_off = col_idx - cols[:, None] + (NA_KC - 1)
    scale = DC ** -0.5

    def row(args):
        q_r, r = args
        rstart = jnp.clip(r - KR // 2, 0, R - KR)
        kr = lax.dynamic_slice_in_dim(k, rstart, KR, axis=2)
        vr = lax.dynamic_slice_in_dim(v, rstart, KR, axis=2)
        kn = kr[:, :, :, col_idx]
        vn = vr[:, :, :, col_idx]
        s = jnp.einsum('bhwd,bhrwcd->bhwrc', q_r, kn).astype(jnp.float32) * scale
        row_off = rstart + jnp.arange(KR) - r + (NA_KR - 1)
        bias = rpb[:, row_off[:, None, None], col_off[None, :, :]]
        s = s + bias.transpose(0, 2, 1, 3)[None].astype(jnp.float32)
        a = jax.nn.softmax(s.reshape(B, HC, GRID_W, KR * NA_KC), axis=-1)
        a = a.reshape(B, HC, GRID_W, KR, NA_KC).astype(vn.dtype)
        return jnp.einsum('bhwrc,bhrwcd->bhwd', a, vn)

    o = lax.map(row, (q, jnp.arange(R)))
    return o.transpose(1, 0, 3, 2, 4).reshape(B, S, WIDTH_C)


def setup_inputs(seed: int = 0) -> dict:
    key = jax.random.key(seed)
    ks = iter(jax.random.split(key, 32))
    L, D = DEPTH, D_MODEL

    def nrm(shape, scale):
        return jax.random.normal(next(ks), shape, jnp.float32) * scale

    def gain(shape):
        return 1.0 + nrm(shape, 0.02)

    return {
        "x": nrm((BATCH, SEQ, D), 1.0),
        "p": nrm((DEPTH, BATCH, SEQ, PLE_DIM), 1.0),
        "norm_mix": gain((L, D)),
        "w_in": nrm((L, D, D_IN), D ** -0.5),
        "lam_q1": nrm((L, DH_A), 0.1),
        "lam_k1": nrm((L, DH_A), 0.1),
        "lam_q2": nrm((L, DH_A), 0.1),
        "lam_k2": nrm((L, DH_A), 0.1),
        "subln_g": gain((L, DV_A)),
        "conv_w": nrm((L, CONV_W, 1, CB), CONV_W ** -0.5),
        "conv_b": nrm((L, CB), 0.02),
        "cln_g": gain((L, CB)),
        "cln_b": nrm((L, CB), 0.02),
        "rpb": nrm((L, HC, 2 * NA_KR - 1, 2 * NA_KC - 1), 0.02),
        "w_br_a": nrm((L, WIDTH_A, D), WIDTH_A ** -0.5),
        "w_br_b": nrm((L, CB, D), CB ** -0.5),
        "w_br_c": nrm((L, WIDTH_C, D), WIDTH_C ** -0.5),
        "w_o": nrm((L, D, D), D ** -0.5),
        "norm_ffn": gain((L, D)),
        "w_ffn_gate": nrm((L, D, D_FF), D ** -0.5),
        "w_ffn_up": nrm((L, D, D_FF), D ** -0.5),
        "w_ffn_down": nrm((L, D_FF, D), D_FF ** -0.5),
        "norm_ple": gain((L, D)),
        "w_ple_in": nrm((L, PLE_DIM, D), PLE_DIM ** -0.5),
        "w_ple_gate": nrm((L, D, D), D ** -0.5),
        "norm_final": gain((D,)),
    }


def reference(x, p, norm_mix, w_in, lam_q1, lam_k1, lam_q2, lam_k2, subln_g,
              conv_w, conv_b, cln_g, cln_b, rpb, w_br_a, w_br_b, w_br_c, w_o,
              norm_ffn, w_ffn_gate, w_ffn_up, w_ffn_down, norm_ple, w_ple_in,
              w_ple_gate, norm_final):
    S = x.shape[1]
    cos, sin = rope_tables(S, DH_A)
    for i in range(DEPTH):
        lam_init = 0.8 - 0.6 * math.exp(-0.3 * i)
        h = rmsnorm(x, norm_mix[i])
        proj = h @ w_in[i]
        qa, ka, va, glu_in, qc, kc, vc, gates = jnp.split(proj, SPLITS, axis=-1)
        ya = diff_attention(qa, ka, va, lam_q1[i], lam_k1[i], lam_q2[i], lam_k2[i],
                            subln_g[i], lam_init, cos, sin)
        yb = conformer_conv(glu_in, conv_w[i], conv_b[i], cln_g[i], cln_b[i])
        yc = neighbourhood_attention(qc, kc, vc, rpb[i])
        ga, gb, gc = jnp.split(jax.nn.sigmoid(gates), 3, axis=-1)
        merged = (ga * (ya @ w_br_a[i]) + gb * (yb @ w_br_b[i])
                  + gc * (yc @ w_br_c[i]))
        x = x + merged @ w_o[i]
        h = rmsnorm(x, norm_ffn[i])
        x = x + (jax.nn.silu(h @ w_ffn_gate[i]) * (h @ w_ffn_up[i])) @ w_ffn_down[i]
        h = rmsnorm(x, norm_ple[i])
        x = x + jax.nn.sigmoid(h @ w_ple_gate[i]) * (p[i] @ w_ple_in[i])
    return rmsnorm(x, norm_final)
```

```python
import math
import numpy as np
from contextlib import ExitStack
import concourse.bass as bass
import concourse.mybir as mybir
from concourse.bass_utils import run_bass_kernel_spmd

F32 = mybir.dt.float32
BF16 = mybir.dt.bfloat16
AF = mybir.ActivationFunctionType
ALU = mybir.AluOpType
AX = mybir.AxisListType

EPS = 1e-6
NEG = -30000.0
ENGS = ("pe", "act", "dve", "pool", "sp")
NSM = 545
NTAB = 27


class Slot:
    def __init__(self, sem):
        self.sem = sem
        self.count = 0


class Sched:
    def __init__(self, nc, stack):
        self.nc = nc
        self.eobj = {"pe": nc.tensor, "act": nc.scalar, "dve": nc.vector, "pool": nc.gpsimd, "sp": nc.sync}
        self.stack = stack
        self.cnt = {e: 0 for e in ENGS}
        self.esem = {e: stack.enter_context(nc.semaphore("s_" + e)) for e in ENGS}
        self.known = {e: {} for e in ENGS}
        self.res = {}
        self.slots = []
        self.semh = {}
        for e in ENGS:
            self.semh[self.esem[e].num] = self.esem[e]

    def slot(self):
        s = self.stack.enter_context(self.nc.semaphore("dq%d" % len(self.slots)))
        self.semh[s.num] = s
        sl = Slot(s)
        self.slots.append(sl)
        return sl

    def _r(self, key):
        r = self.res.get(key)
        if r is None:
            r = {"w": None, "r": {}}
            self.res[key] = r
        return r

    def _deps(self, eng, reads, writes):
        waits = {}
        own = self.esem[eng].num

        def add(t, same_ok):
            if t is None:
                return
            s, v = t
            if s == own and not same_ok:
                return
            if v > waits.get(s, 0):
                waits[s] = v

        for k in reads:
            add(self._r(k)["w"], True)
        for k in writes:
            r = self._r(k)
            add(r["w"], eng != "pe")
            for s, v in r["r"].items():
                add((s, v), eng != "pe")
        kn = self.known[eng]
        out = []
        for s, v in waits.items():
            if kn.get(s, 0) < v:
                kn[s] = v
                out.append((s, v))
        return out

    def _commit(self, tick, reads, writes):
        s, v = tick
        for k in reads:
            r = self._r(k)
            if r["r"].get(s, 0) < v:
                r["r"][s] = v
        for k in writes:
            r = self._r(k)
            r["w"] = tick
            r["r"] = {}

    dead = False
    nops = 0
    limit = None

    def _lim(self):
        self.nops += 1
        if self.limit is not None and self.nops > self.limit:
            self.dead = True

    @staticmethod
    def _excl(key):
        name = key if isinstance(key, str) else key[0]
        return isinstance(name, str) and (name.startswith("ps") or name in ("acc", "accn", "S", "Sn"))

    def _split(self, reads, writes):
        ex = [k for k in reads if self._excl(k)]
        if not ex:
            return reads, writes
        return [k for k in reads if not self._excl(k)], list(writes) + ex

    def op(self, eng, fn, reads=(), writes=(), inc=True):
        self._lim()
        if self.dead:
            return
        reads, writes = self._split(reads, writes)
        waits = self._deps(eng, reads, writes)
        if inc:
            self.cnt[eng] += 1
            tick = (self.esem[eng].num, self.cnt[eng])
        else:
            tick = (self.esem[eng].num, self.cnt[eng] + 1)
        self._commit(tick, reads, writes)
        engine = self.eobj[eng]
        for s, v in waits:
            engine.wait_ge(self.semh[s], v)
        ins = fn(engine)
        if inc:
            ins.then_inc(self.esem[eng], 1)

    def dma(self, eng, slot, fn, reads=(), writes=(), amt=16):
        self._lim()
        if self.dead:
            return
        waits = self._deps(eng, reads, writes)
        slot.count += amt
        tick = (slot.sem.num, slot.count)
        self._commit(tick, reads, writes)
        engine = self.eobj[eng]
        for s, v in waits:
            engine.wait_ge(self.semh[s], v)
        fn(engine).then_inc(slot.sem, amt)

    def barrier(self):
        if self.dead:
            return
        for e in ENGS:
            engine = self.eobj[e]
            kn = self.known[e]
            for f in ENGS:
                if f == e:
                    continue
                s = self.esem[f]
                if kn.get(s.num, 0) < self.cnt[f]:
                    kn[s.num] = self.cnt[f]
                    engine.wait_ge(s, self.cnt[f])
            for sl in self.slots:
                if kn.get(sl.sem.num, 0) < sl.count:
                    kn[sl.sem.num] = sl.count
                    engine.wait_ge(sl.sem, sl.count)


class Buf:
    def __init__(self, t, key, slot=None):
        self.t = t
        self.key = key
        self.slot = slot


class C:
    pass


class _Stop(Exception):
    pass


def sbuf(c, ph, name, shape, dt):
    c.uid += 1
    return ph.enter_context(c.nc.sbuf_tensor("%s_%d" % (name, c.uid), shape, dt))


def psum(c, ph, name, shape, dt):
    c.uid += 1
    return ph.enter_context(c.nc.psum_tensor("%s_%d" % (name, c.uid), shape, dt))


def mk_ring(c, ph, name, n, cols):
    return {"bufs": [Buf(sbuf(c, ph, name, [128, cols], BF16), (name, c.uid, i), c.S.slot()) for i in range(n)], "i": 0}


def wget(c, ring, src3d, k, n, eng="pool"):
    b = ring["bufs"][ring["i"] % len(ring["bufs"])]
    ring["i"] += 1
    view = b.t[:, 0:k * n].rearrange("p (k n) -> p k n", n=n)
    c.S.dma(eng, b.slot, lambda e: e.dma_start(out=view, in_=src3d), writes=[b.key])
    return view, b.key


def wview(w2d, c0, c1):
    return w2d[:, c0:c1].rearrange("(k p) n -> p k n", p=128)


def rms_chunk(c, xsrc, xkeys, n, gain, h, hkey, sq=None):
    S = c.S
    keys = [(hkey, k) for k in range(8)]
    if sq is None:
        sq, sqkeys = h, keys
    else:
        sqkeys = ["sqx"]
    S.op("act", lambda e: e.activation(sq[:, :, 0:n], xsrc, AF.Square), reads=xkeys, writes=sqkeys)
    psm = c.ps_rms[:, 0:n]
    for k in range(8):
        S.op("pe", lambda e, k=k: e.matmul(psm, c.onesD[:], sq[:, k, 0:n], start=(k == 0), stop=(k == 7)),
             reads=sqkeys, writes=["ps_rms"], inc=(k == 7))
    S.op("act", lambda e: e.activation(c.sd[:, 0:n], psm, AF.Sqrt, bias=c.epsc[:, 0:1], scale=1.0), reads=["ps_rms"], writes=["sd"])
    S.op("dve", lambda e: e.reciprocal(c.rstd[:, 0:n], c.sd[:, 0:n]), reads=["sd"], writes=["rstd"])
    for k in range(8):
        S.op("dve", lambda e, k=k: e.scalar_tensor_tensor(h[:, k, 0:n], xsrc[:, k, :], gain[:, k:k + 1], c.rstd[:, 0:n], ALU.mult, ALU.mult),
             reads=list(xkeys) + ["rstd"], writes=[(hkey, k)])
    return keys


def proj_fm(c, out_ps, pskey, W, wkey, col0, h, hkeys, n, nk=8, inc_last=True):
    for k in range(nk):
        c.S.op("pe", lambda e, k=k: e.matmul(out_ps, W[:, k, col0:col0 + 128], h[:, k, 0:n], start=(k == 0), stop=(k == nk - 1)),
               reads=[wkey] + list(hkeys), writes=[pskey], inc=(k == nk - 1))


def rope_evac(c, ps, pskey, psr, psrkey, cosb, sinb, ckey, out_ap, outkey, n, i):
    S = c.S
    ib, it_ = i % len(c.kb), i % len(c.t1)
    kb, t1, t2 = c.kb[ib], c.t1[it_], c.t2[it_]
    kk, k1, k2 = ("kb", ib), ("t1", it_), ("t2", it_)
    S.op("act", lambda e: e.activation(kb[:, 0:n], ps, AF.Copy), reads=[pskey], writes=[kk])
    S.op("pe", lambda e: e.matmul(psr, c.rot[:], kb[:, 0:n], start=True, stop=True), reads=[kk], writes=[psrkey])
    S.op("dve", lambda e: e.tensor_tensor(t1[:, 0:n], ps, cosb, ALU.mult), reads=[pskey, ckey], writes=[k1])
    S.op("dve", lambda e: e.tensor_tensor(t2[:, 0:n], psr, sinb, ALU.mult), reads=[psrkey, ckey], writes=[k2])
    S.op("pool", lambda e: e.tensor_tensor(out_ap, t1[:, 0:n], t2[:, 0:n], ALU.add), reads=[k1, k2], writes=[outkey])


def build(layer_ids, final, exchange, dbg=None, stop=None):
    nl = len(layer_ids)
    nc = bass.Bass("TRN2", target_bir_lowering=False)
    c = C()
    c.nc = nc
    c.uid = 0
    dr = {}

    def din(name, shape):
        dr[name] = nc.dram_tensor(name, shape, F32, kind="ExternalInput").ap()

    din("xown", [1024, 2048])
    din("xg", [2, 1024, 2048])
    din("pT", [nl, 256, 2048])
    din("w_in", [nl, 1024, 7168])
    din("w_br", [nl, 3, 512, 1024])
    din("w_o", [nl, 1024, 1024])
    din("w_fg", [nl, 1024, 2816])
    din("w_fu", [nl, 1024, 2816])
    din("w_fd", [nl, 2816, 1024])
    din("w_pi", [nl, 256, 1024])
    din("w_pg", [nl, 1024, 1024])
    din("sm", [nl, 128, NSM])
    din("sm2", [128, 16])
    din("cosg", [128, 4096])
    din("sing", [128, 4096])
    din("coso", [128, 2048])
    din("sino", [128, 2048])
    din("ctab", [2, 128, 128])
    din("nbt", [nl, 8, 128, NTAB * 128])
    yT = nc.dram_tensor("yT", [1024, 2048], F32, kind="ExternalOutput").ap()
    dbgo = {}
    if dbg:
        for nm in dbg:
            dbgo[nm] = nc.dram_tensor("dbg_" + nm, [512, 2048], F32, kind="ExternalOutput").ap()
    if exchange:
        xs_d = [nc.dram_tensor("xs_d%d" % q, [1024, 512], F32, kind="Internal").ap() for q in range(4)]
        xg_d = [nc.dram_tensor("xg_d%d" % q, [2 * 1024, 512], F32, kind="Internal").ap() for q in range(4)]

    with ExitStack() as top:
        S = Sched(nc, top)
        c.S = S
        if isinstance(stop, int):
            S.limit = stop
        xT = sbuf(c, top, "xT", [128, 8, 2048], F32)
        c.xT = xT
        smt = sbuf(c, top, "smt", [128, nl, NSM], F32)
        sm2 = sbuf(c, top, "sm2", [128, 16], F32)
        c.ident = sbuf(c, top, "ident", [128, 128], BF16)
        c.rot = sbuf(c, top, "rot", [128, 128], BF16)
        c.onesD = sbuf(c, top, "onesD", [128, 128], BF16)
        c.onesC = sbuf(c, top, "onesC", [128, 128], BF16)
        c.ones1 = sbuf(c, top, "ones1", [128, 128], BF16)
        c.onesL = sbuf(c, top, "onesL", [128, 128], BF16)
        c.epsc = sbuf(c, top, "epsc", [128, 1], F32)
        c.sd = sbuf(c, top, "sd", [128, 512], F32)
        c.rstd = sbuf(c, top, "rstd", [128, 512], F32)
        lamt = sbuf(c, top, "lamt", [128, 16], F32)
        gsc4 = sbuf(c, top, "gsc4", [128, 4, 128], F32)
        ld = S.slot()
        ldp = S.slot()
        xslot = S.slot()
        oslot = S.slot()
        oslotp = S.slot()

        for k in range(8):
            S.dma("sp", ld, lambda e, k=k: e.dma_start(out=xT[:, k, :], in_=dr["xown"][k * 128:(k + 1) * 128, :]), writes=["xT_all"])
        S.dma("sp", ld, lambda e: e.dma_start(out=smt[:], in_=dr["sm"].rearrange("l p n -> p l n")), writes=["smt"])
        S.dma("sp", ld, lambda e: e.dma_start(out=sm2[:], in_=dr["sm2"]), writes=["sm2"])
        S.dma("pool", ldp, lambda e: e.dma_start(out=c.ident[:], in_=dr["ctab"][0]), writes=["ident"])
        S.dma("pool", ldp, lambda e: e.dma_start(out=c.rot[:], in_=dr["ctab"][1]), writes=["rot"])
        S.op("dve", lambda e: e.memset(c.onesD[:], 1.0 / 1024.0), writes=["onesD"])
        S.op("dve", lambda e: e.memset(c.onesC[:], 1.0 / 512.0), writes=["onesC"])
        S.op("dve", lambda e: e.memset(c.ones1[:], 1.0), writes=["ones1"])
        S.op("dve", lambda e: e.memset(c.onesL[:], 1.0 / 128.0), writes=["onesL"])
        S.op("dve", lambda e: e.memset(c.epsc[:], EPS), writes=["epsc"])
        S.barrier()

        def xkeys(g):
            return [("x", k, g) for k in range(8)]

        hstate = {"bufs": [], "i": 0}

        def mk_h(ph, n, count):
            hstate["bufs"] = [sbuf(c, ph, "h", [128, 8, n], BF16) for _ in range(count)]
            hstate["i"] = 0

        def next_h():
            hstate["i"] += 1
            j = hstate["i"] % len(hstate["bufs"])
            return hstate["bufs"][j], ("h", j)

        def chk(name):
            import os as _os
            if _os.environ.get("KDEBUG"):
                print("phase", name, "nops", S.nops)
            if stop == name:
                S.dead = True

        for li, lid in enumerate(layer_ids):
          try:
            lam_init = 0.8 - 0.6 * math.exp(-0.3 * lid)
            sml = smt[:, li, :]
            g_mix = sml[:, 0:8]
            g_ffn = sml[:, 8:16]
            g_ple = sml[:, 16:24]
            w_in = dr["w_in"][li]
            xg = dr["xg"] if (li == 0 or not exchange) else None

            def xg_view(half, t0, n):
                if xg is not None:
                    return xg[half][:, t0:t0 + n].rearrange("(k p) t -> p k t", p=128)
                q, off = t0 // 512, t0 % 512
                assert off + n <= 512
                return xg_d[q][half * 1024:(half + 1) * 1024, off:off + n].rearrange("(k p) t -> p k t", p=128)

            S.op("dve", lambda e: e.tensor_tensor(gsc4[:, 0, 0:64], sml[:, 288:352], sml[:, 352:416], ALU.mult), reads=["smt"], writes=["lt0"])
            S.op("dve", lambda e: e.reduce_sum(lamt[:, 0:1], gsc4[:, 0, 0:64], AX.X), reads=["lt0"], writes=["lam0"])
            S.op("dve", lambda e: e.tensor_tensor(gsc4[:, 1, 0:64], sml[:, 416:480], sml[:, 480:544], ALU.mult), reads=["smt"], writes=["lt1"])
            S.op("dve", lambda e: e.reduce_sum(lamt[:, 1:2], gsc4[:, 1, 0:64], AX.X), reads=["lt1"], writes=["lam1"])
            S.op("act", lambda e: e.activation(lamt[:, 2:4], lamt[:, 0:2], AF.Exp), reads=["lam0", "lam1"], writes=["lam2"])
            S.op("dve", lambda e: e.tensor_tensor(lamt[:, 5:6], lamt[:, 3:4], lamt[:, 2:3], ALU.subtract), reads=["lam2"], writes=["lam3"])
            S.op("dve", lambda e: e.tensor_scalar(lamt[:, 4:5], lamt[:, 5:6], -lam_init, None, ALU.add), reads=["lam3"], writes=["nlam"])
            for q in range(4):
                S.op("dve", lambda e, q=q: e.tensor_scalar(gsc4[:, q, :], sml[:, 160:288], 1.0 - lam_init, None, ALU.mult),
                     reads=["smt", "lam0", "lam1"], writes=[("gsc4", q)])
            S.op("dve", lambda e: e.tensor_scalar(lamt[:, 6:7], sml[:, 544:545], 1.0 - lam_init, None, ALU.mult), reads=["smt"], writes=["gcol"])
            nlam = lamt[:, 4:5]
            gcol = lamt[:, 6:7]
            S.barrier()
            chk("init")

            with ExitStack() as Ly:
                yaT = sbuf(c, Ly, "yaT", [128, 4, 2048], BF16)
                with ExitStack() as A:
                    KT = sbuf(c, A, "KT", [128, 4, 4096], BF16)
                    V = sbuf(c, A, "V", [128, 32, 4, 129], BF16)
                    c.kb = [sbuf(c, A, "kb", [128, 512], BF16) for _ in range(2)]
                    c.t1 = [sbuf(c, A, "t1", [128, 512], F32)]
                    c.t2 = [sbuf(c, A, "t2", [128, 512], F32)]
                    cslot = [S.slot() for _ in range(2)]
                    with ExitStack() as ph:
                        wr = mk_ring(c, ph, "wA1", 2, 4096)
                        xs = sbuf(c, ph, "xs", [128, 8, 256], F32)
                        cosb = [sbuf(c, ph, "cosb", [128, 256], F32) for _ in range(2)]
                        sinb = [sbuf(c, ph, "sinb", [128, 256], F32) for _ in range(2)]
                        mk_h(ph, 256, 2)
                        c.ps_rms = psum(c, ph, "ps_rms", [128, 512], F32)
                        psk = [psum(c, ph, "psk", [128, 512], F32) for _ in range(2)]
                        psr = [psum(c, ph, "psr", [128, 512], F32) for _ in range(2)]
                        psv = [psum(c, ph, "psv", [128, 512], F32) for _ in range(2)]
                        Wk, wkk = wget(c, wr, wview(w_in, 512, 1024), 8, 512)
                        Wv, wvk = wget(c, wr, wview(w_in, 1024, 1536), 8, 512)
                        S.op("pool", lambda e: e.memset(V[:, :, :, 128:129], 1.0), writes=["Vones"])
                        it = 0
                        for ci in range(16):
                            half, t0 = ci // 8, (ci % 8) * 256
                            S.dma("sp", xslot, lambda e: e.dma_start(out=xs[:], in_=xg_view(half, t0, 256)), writes=["xs"])
                            cb = ci % 2
                            S.dma("sp", cslot[cb], lambda e: e.dma_start(out=cosb[cb][:], in_=dr["cosg"][:, ci * 256:(ci + 1) * 256]), writes=[("cos", cb)])
                            S.dma("sp", cslot[cb], lambda e: e.dma_start(out=sinb[cb][:], in_=dr["sing"][:, ci * 256:(ci + 1) * 256]), writes=[("cos", cb)])
                            h, hk0 = next_h()
                            hk = rms_chunk(c, xs[:], ["xs"], 256, g_mix, h, hk0)
                            for hd in range(4):
                                pk, pr = psk[it % 2], psr[it % 2]
                                proj_fm(c, pk[:, 0:256], ("psk", it % 2), Wk, wkk, hd * 128, h, hk, 256)
                                rope_evac(c, pk[:, 0:256], ("psk", it % 2), pr[:, 0:256], ("psr", it % 2), cosb[cb][:], sinb[cb][:], ("cos", cb),
                                          KT[:, hd, ci * 256:(ci + 1) * 256], ("KT", hd, ci // 2), 256, it)
                                it += 1
                            for j in range(2):
                                pv = psv[j % 2]
                                for k in range(8):
                                    S.op("pe", lambda e, k=k, j=j, pv=pv: e.matmul(pv[:], h[:, k, j * 128:(j + 1) * 128], Wv[:, k, :], start=(k == 0), stop=(k == 7)),
                                         reads=[wvk] + hk, writes=[("psv", j % 2)], inc=(k == 7))
                                kc = ci * 2 + j
                                S.op("act", lambda e, kc=kc, pv=pv: e.activation(V[:, kc, :, 0:128], pv[:].rearrange("p (h d) -> p h d", d=128), AF.Copy),
                                     reads=[("psv", j % 2)], writes=[("V", kc)])
                        S.barrier()
                        chk("A1")
                    with ExitStack() as ph:
                        wr = mk_ring(c, ph, "wA2", 1, 4096)
                        mk_h(ph, 512, 1)
                        cosq = sbuf(c, ph, "cosq", [128, 512], F32)
                        sinq = sbuf(c, ph, "sinq", [128, 512], F32)
                        Q = sbuf(c, ph, "QT", [128, 4, 512], BF16)
                        PT = [sbuf(c, ph, "PT", [128, 512], BF16) for _ in range(4)]
                        B = [sbuf(c, ph, "B", [128, 512], F32) for _ in range(4)]
                        accPb = [sbuf(c, ph, "accPb", [128, 512], BF16) for _ in range(2)]
                        sqb = sbuf(c, ph, "sqb", [128, 512], BF16)
                        accO = [psum(c, ph, "accO", [128, 512], F32) for _ in range(2)]
                        Sps = [[psum(c, ph, "Sps", [128, 512], F32) for _ in range(2)] for _ in range(2)]
                        c.ps_rms = psum(c, ph, "ps_misc", [128, 512], F32)
                        psR = [psum(c, ph, "psR", [128, 512], F32), c.ps_rms]
                        psRk = [("psR", 0), "ps_rms"]
                        Wq, wqk = wget(c, wr, wview(w_in, 0, 512), 8, 512)
                        it = 0
                        pti = 0
                        Bk = [("B", i) for i in range(4)]
                        for g in range(4):
                            S.dma("sp", cslot[0], lambda e: e.dma_start(out=cosq[:], in_=dr["coso"][:, g * 512:(g + 1) * 512]), writes=["cosq"])
                            S.dma("sp", cslot[0], lambda e: e.dma_start(out=sinq[:], in_=dr["sino"][:, g * 512:(g + 1) * 512]), writes=["cosq"])
                            h, hk0 = next_h()
                            hk = rms_chunk(c, xT[:, :, g * 512:(g + 1) * 512], xkeys(g) + ["xT_all"], 512, g_mix, h, hk0)
                            for hd in range(4):
                                proj_fm(c, c.ps_rms[:], "ps_rms", Wq, wqk, hd * 128, h, hk, 512)
                                rope_evac(c, c.ps_rms[:], "ps_rms", psR[0][:], ("psR", 0), cosq[:], sinq[:], "cosq",
                                          Q[:, hd, :], ("QT", hd), 512, it)
                                it += 1
                            for hd in range(4):
                                S.op("dve", lambda e: e.memset(B[0][:], 0.0), writes=[Bk[0]])

                                def qkpair(kc):
                                    for cc in range(2):
                                        sp = Sps[cc][kc % 2]
                                        S.op("pe", lambda e, cc=cc, sp=sp: e.matmul(sp[:], KT[cc * 64:(cc + 1) * 64, hd, kc * 128:(kc + 1) * 128],
                                                                                   Q[cc * 64:(cc + 1) * 64, hd, :], start=True, stop=True),
                                             reads=[("KT", hd, kc // 4), ("QT", hd)], writes=[("S", cc, kc % 2)])

                                def av(kc, cc, pidx):
                                    sp = Sps[cc][kc % 2]
                                    pt = PT[pidx % 4]
                                    pk_ = ("PT", pidx % 4)
                                    S.op("act", lambda e: e.activation(pt[:], sp[:], AF.Exp, scale=0.125), reads=[("S", cc, kc % 2)], writes=[pk_])
                                    S.op("pe", lambda e: e.matmul(accO[cc][:], V[:, kc, hd, 0:128], pt[:], start=(kc == 0), stop=(kc == 31)),
                                         reads=[pk_, ("V", kc)], writes=[("accO", cc)])
                                    if cc == 1:
                                        S.op("pe", lambda e: e.matmul(c.ps_rms[:], c.ones1[:], pt[:], start=(kc == 0), stop=(kc == 31)),
                                             reads=[pk_, "ones1"], writes=["ps_rms"])
                                    else:
                                        S.op("dve", lambda e: e.tensor_tensor(B[0][:], B[0][:], pt[:], ALU.add), reads=[pk_, Bk[0]], writes=[Bk[0]])

                                qkpair(0)
                                for kc in range(32):
                                    if kc + 1 < 32:
                                        qkpair(kc + 1)
                                    for cc in range(2):
                                        av(kc, cc, pti)
                                        pti += 1
                                S.op("dve", lambda e: e.tensor_copy(accPb[0][:], B[0][:]), reads=[Bk[0]], writes=[("accPb", 0)])
                                S.op("pe", lambda e: e.matmul(psR[0][:], c.ones1[:], accPb[0][:], start=True, stop=True), reads=[("accPb", 0), "ones1"], writes=[psRk[0]])
                                for cc in range(2):
                                    S.op("act", lambda e, cc=cc: e.activation(B[cc][:], psR[cc][:], AF.Ln), reads=[psRk[cc], ("accPb", 0)], writes=[Bk[cc]])
                                    S.op("act", lambda e, cc=cc: e.activation(B[cc][:], B[cc][:], AF.Exp, scale=-1.0), reads=[Bk[cc]], writes=[Bk[cc]])
                                S.op("dve", lambda e: e.tensor_tensor(B[2][:], accO[0][:], B[0][:], ALU.mult), reads=[("accO", 0), Bk[0], ("accPb", 0)], writes=[Bk[2]])
                                S.op("dve", lambda e: e.tensor_tensor(B[3][:], accO[1][:], B[1][:], ALU.mult), reads=[("accO", 1), Bk[1], ("accPb", 1)], writes=[Bk[3]])
                                S.op("dve", lambda e: e.scalar_tensor_tensor(B[0][:], B[3][:], nlam, B[2][:], ALU.mult, ALU.add), reads=[Bk[3], Bk[2], "nlam"], writes=[Bk[0]])
                                S.op("pool", lambda e: e.tensor_tensor(sqb[:], B[0][:], B[0][:], ALU.mult), reads=[Bk[0]], writes=["sqb"])
                                S.op("pe", lambda e: e.matmul(psR[0][:], c.onesL[:], sqb[:], start=True, stop=True), reads=["sqb", "onesL"], writes=[("psR", 0)])
                                S.op("act", lambda e: e.activation(B[1][:], psR[0][:], AF.Ln, bias=c.epsc[:, 0:1], scale=1.0), reads=[("psR", 0)], writes=[Bk[1]])
                                S.op("act", lambda e: e.activation(B[1][:], B[1][:], AF.Exp, scale=-0.5), reads=[Bk[1]], writes=[Bk[1]])
                                S.op("dve", lambda e: e.scalar_tensor_tensor(yaT[:, hd, g * 512:(g + 1) * 512], B[0][:], gcol, B[1][:], ALU.mult, ALU.mult),
                                     reads=[Bk[0], Bk[1], "gcol"], writes=[("yaT", g)])
                        S.barrier()

                ycT = sbuf(c, Ly, "ycT", [128, 4, 2048], BF16)
                with ExitStack() as Cn:
                    kcT = sbuf(c, Cn, "kcT", [128, 4, 2560], BF16)
                    Vc = sbuf(c, Cn, "Vc", [128, 20, 8, 65], BF16)
                    qcT = sbuf(c, Cn, "qcT", [128, 4, 2048], BF16)
                    with ExitStack() as ph:
                        wr = mk_ring(c, ph, "wC1", 3, 4096)
                        xs = sbuf(c, ph, "xs", [128, 8, 256], F32)
                        mk_h(ph, 256, 2)
                        c.ps_rms = psum(c, ph, "ps_rms", [128, 512], F32)
                        psq = [psum(c, ph, "psq", [128, 512], F32) for _ in range(3)]
                        psv = [psum(c, ph, "psv", [128, 512], F32) for _ in range(2)]
                        Wq, wqk = wget(c, wr, wview(w_in, 2560, 3072), 8, 512)
                        Wk, wkk = wget(c, wr, wview(w_in, 3072, 3584), 8, 512)
                        Wv, wvk = wget(c, wr, wview(w_in, 3584, 4096), 8, 512)
                        S.op("pool", lambda e: e.memset(Vc[:, :, :, 64:65], 1.0), writes=["Vcones"])
                        pi = 0
                        for ci in range(10):
                            h, hk0 = next_h()
                            own = ci < 8
                            if own:
                                hk = rms_chunk(c, xT[:, :, ci * 256:(ci + 1) * 256], xkeys(ci // 2) + ["xT_all"], 256, g_mix, h, hk0)
                                kcol, rpb_ = 256 + ci * 256, 2 + ci * 2
                            else:
                                if ci == 8:
                                    S.dma("sp", xslot, lambda e: e.dma_start(out=xs[:], in_=xg_view(0, 1792, 256)), writes=["xs"])
                                    kcol, rpb_ = 0, 0
                                else:
                                    S.dma("sp", xslot, lambda e: e.dma_start(out=xs[:], in_=xg_view(1, 0, 256)), writes=["xs"])
                                    kcol, rpb_ = 2304, 18
                                hk = rms_chunk(c, xs[:], ["xs"], 256, g_mix, h, hk0)
                            for hp in range(4):
                                if own:
                                    pq = psq[pi % 3]
                                    proj_fm(c, pq[:, 0:256], ("psq", pi % 3), Wq, wqk, hp * 128, h, hk, 256)
                                    S.op("act", lambda e, pq=pq, hp=hp: e.activation(qcT[:, hp, ci * 256:(ci + 1) * 256], pq[:, 0:256], AF.Copy),
                                         reads=[("psq", pi % 3)], writes=[("qcT", ci // 2)])
                                    pi += 1
                                pq = psq[pi % 3]
                                proj_fm(c, pq[:, 0:256], ("psq", pi % 3), Wk, wkk, hp * 128, h, hk, 256)
                                S.op("dve", lambda e, pq=pq, hp=hp: e.tensor_copy(kcT[:, hp, kcol:kcol + 256], pq[:, 0:256]),
                                     reads=[("psq", pi % 3)], writes=[("kcT", ci)])
                                pi += 1
                            for j in range(2):
                                pv = psv[j % 2]
                                for k in range(8):
                                    S.op("pe", lambda e, k=k, j=j, pv=pv: e.matmul(pv[:], h[:, k, j * 128:(j + 1) * 128], Wv[:, k, :], start=(k == 0), stop=(k == 7)),
                                         reads=[wvk] + hk, writes=[("psv", j % 2)], inc=(k == 7))
                                rp = rpb_ + j
                                S.op("act", lambda e, rp=rp, pv=pv: e.activation(Vc[:, rp, :, 0:64], pv[:].rearrange("p (h d) -> p h d", d=64), AF.Copy),
                                     reads=[("psv", j % 2)], writes=[("Vc", rp)])
                        S.barrier()
                        chk("Cn1")
                    with ExitStack() as ph:
                        tabs = [sbuf(c, ph, "tab", [128, NTAB, 128], F32) for _ in range(2)]
                        tslot = [S.slot() for _ in range(2)]
                        tmp = [sbuf(c, ph, "tmp", [128, 6, 128], F32) for _ in range(2)]
                        PTn = [sbuf(c, ph, "PTn", [128, 6, 128], BF16) for _ in range(2)]
                        rrn = [sbuf(c, ph, "rrn", [128, 1], F32) for _ in range(2)]
                        yct = [sbuf(c, ph, "yct", [128, 128], BF16) for _ in range(2)]
                        accn = psum(c, ph, "accn", [128, 2, 4, 128], F32)
                        Sn = [psum(c, ph, "Sn", [128, 8, 128], F32) for _ in range(2)]
                        psT = psum(c, ph, "psT", [128, 1024], BF16)
                        nbt = dr["nbt"][li]
                        items = [(hp, m, eh) for hp in range(4) for m in range(16) for eh in range(2)]

                        def geom(m):
                            if m == 0:
                                return 0, 6, 0
                            if m == 1:
                                return 1, 5, 6
                            if m == 14:
                                return 14, 5, 16
                            if m == 15:
                                return 14, 6, 21
                            return m, 5, 11

                        def stage1(it):
                            hp, m, eh = items[it]
                            if m == 0 and eh == 0:
                                for e2 in range(2):
                                    S.dma("sp", tslot[e2], lambda e, e2=e2: e.dma_start(out=tabs[e2][:], in_=nbt[2 * hp + e2].rearrange("p (t q) -> p t q", q=128)),
                                          writes=[("tab", e2)])
                            rp0, nch, tb = geom(m)
                            tbuf, tkey = tabs[eh], ("tab", eh)
                            sn, skey = Sn[it % 2], ("Sn", it % 2)
                            for ch in range(nch):
                                rp = rp0 + ch
                                S.op("pe", lambda e, ch=ch, rp=rp: e.matmul(sn[:, ch, :], kcT[eh * 64:(eh + 1) * 64, hp, rp * 128:(rp + 1) * 128],
                                                                            qcT[eh * 64:(eh + 1) * 64, hp, m * 128:(m + 1) * 128], start=True, stop=True),
                                     reads=[("kcT", kk) for kk in range(10)] + [("qcT", m // 4)], writes=[skey], inc=(ch == nch - 1))
                            tm, mkey = tmp[it % 2], ("tmp", it % 2)
                            S.op("dve", lambda e: e.scalar_tensor_tensor(tm[:, 0:4, :], sn[:, 0:4, :], 0.125, tbuf[:, tb:tb + 4, :], ALU.mult, ALU.add),
                                 reads=[skey, tkey], writes=[mkey])
                            S.op("dve", lambda e: e.scalar_tensor_tensor(tm[:, 4:nch, :], sn[:, 4:nch, :], 0.125, tbuf[:, tb + 4:tb + nch, :], ALU.mult, ALU.add),
                                 reads=[skey, tkey], writes=[(mkey, 1)])
                            pn, pkey = PTn[it % 2], ("PTn", it % 2)
                            S.op("act", lambda e: e.activation(pn[:, 0:nch, :], tm[:, 0:nch, :], AF.Exp), reads=[mkey, (mkey, 1)], writes=[pkey])

                        def stage2(it):
                            hp, m, eh = items[it]
                            hd = 2 * hp + eh
                            rp0, nch, tb = geom(m)
                            pn, pkey = PTn[it % 2], ("PTn", it % 2)
                            ti = it // 2
                            yc_, ykey = yct[ti % 2], ("yct", ti % 2)
                            asl = accn[:, it % 2, (it // 2) % 4, 0:65]
                            akey = ("accn", it % 2)
                            for ch in range(nch):
                                rp = rp0 + ch
                                S.op("pe", lambda e, ch=ch, rp=rp: e.matmul(asl, pn[:, ch, :], Vc[:, rp, hd, :], start=(ch == 0), stop=(ch == nch - 1)),
                                     reads=[pkey, "Vcones", ("Vc", rp)], writes=[akey], inc=(ch == nch - 1))
                            rn = rrn[it % 2]
                            S.op("dve", lambda e: e.reciprocal(rn[:], accn[:, it % 2, (it // 2) % 4, 64:65]), reads=[akey], writes=[("rrn", it % 2)])
                            S.op("act", lambda e: e.activation(yc_[:, eh * 64:(eh + 1) * 64], accn[:, it % 2, (it // 2) % 4, 0:64], AF.Identity, scale=rn[:, 0:1]),
                                 reads=[akey, ("rrn", it % 2)], writes=[(ykey, eh)])
                            if eh == 1:
                                pcol = (ti % 4) * 128
                                S.op("pe", lambda e: e.transpose(psT[:, pcol:pcol + 128], yc_[:], c.ident[:]), reads=[(ykey, 0), (ykey, 1)], writes=["psT"])
                                S.op("dve", lambda e: e.tensor_copy(ycT[:, hp, m * 128:(m + 1) * 128], psT[:, pcol:pcol + 128]), reads=["psT"], writes=[("ycT", m)])

                        stage1(0)
                        for it in range(len(items)):
                            if it + 1 < len(items):
                                stage1(it + 1)
                            stage2(it)
                        S.barrier()
                        chk("Cn2")

                ybT = sbuf(c, Ly, "ybT", [128, 4, 2048], BF16)
                with ExitStack() as Cb:
                    zT = sbuf(c, Cb, "zT", [128, 4, 2080], BF16)
                    with ExitStack() as ph:
                        wr = mk_ring(c, ph, "wB1", 2, 4096)
                        mk_h(ph, 512, 2)
                        xh = sbuf(c, ph, "xh", [128, 8, 32], F32)
                        sg = [sbuf(c, ph, "sg", [128, 512], F32) for _ in range(2)]
                        tz = sbuf(c, ph, "tz", [128, 32], F32)
                        c.ps_rms = psum(c, ph, "ps_rms", [128, 512], F32)
                        psa = [psum(c, ph, "psa", [128, 512], F32) for _ in range(2)]
                        psg = [psum(c, ph, "psg", [128, 512], F32) for _ in range(2)]
                        Wa, wak = wget(c, wr, wview(w_in, 1536, 2048), 8, 512)
                        Wg, wgk = wget(c, wr, wview(w_in, 2048, 2560), 8, 512)
                        it = 0
                        for ci in range(5):
                            h, hk0 = next_h()
                            if ci < 4:
                                n = 512
                                hk = rms_chunk(c, xT[:, :, ci * 512:(ci + 1) * 512], xkeys(ci) + ["xT_all"], 512, g_mix, h, hk0)
                            else:
                                n = 32
                                S.dma("sp", xslot, lambda e: e.dma_start(out=xh[:, :, 0:16], in_=xg_view(0, 2032, 16)), writes=["xh"])
                                S.dma("sp", xslot, lambda e: e.dma_start(out=xh[:, :, 16:32], in_=xg_view(1, 0, 16)), writes=["xh"])
                                hk = rms_chunk(c, xh[:], ["xh"], 32, g_mix, h, hk0)
                            for cc in range(4):
                                pa, pg = psa[it % 2], psg[it % 2]
                                proj_fm(c, pa[:, 0:n], ("psa", it % 2), Wa, wak, cc * 128, h, hk, n)
                                proj_fm(c, pg[:, 0:n], ("psg", it % 2), Wg, wgk, cc * 128, h, hk, n)
                                sgi = sg[it % 2]
                                S.op("act", lambda e, pg=pg, sgi=sgi: e.activation(sgi[:, 0:n], pg[:, 0:n], AF.Sigmoid), reads=[("psg", it % 2)], writes=[("sg", it % 2)])
                                if ci < 4:
                                    S.op("dve", lambda e, pa=pa, sgi=sgi, cc=cc: e.tensor_tensor(zT[:, cc, 16 + ci * 512:16 + (ci + 1) * 512], pa[:, 0:n], sgi[:, 0:n], ALU.mult),
                                         reads=[("psa", it % 2), ("sg", it % 2)], writes=[("zT", ci)])
                                else:
                                    S.op("dve", lambda e, pa=pa, sgi=sgi: e.tensor_tensor(tz[:], pa[:, 0:32], sgi[:, 0:32], ALU.mult),
                                         reads=[("psa", it % 2), ("sg", it % 2)], writes=["tz"])
                                    S.op("dve", lambda e, cc=cc: e.tensor_scalar(zT[:, cc, 0:16], tz[:, 0:16], sm2[:, 8:9], None, ALU.mult), reads=["tz", "sm2"], writes=[("zT", 4)])
                                    S.op("dve", lambda e, cc=cc: e.tensor_scalar(zT[:, cc, 2064:2080], tz[:, 16:32], sm2[:, 9:10], None, ALU.mult), reads=["tz", "sm2"], writes=[("zT", 5)])
                                it += 1
                        S.barrier()
                        chk("Cb1")
                    with ExitStack() as ph:
                        dg = sbuf(c, ph, "dg", [128, 4, 31, 128], BF16)
                        v32 = sbuf(c, ph, "v32", [128, 4, 512], F32)
                        vb = sbuf(c, ph, "vb", [128, 4, 512], BF16)
                        vsq = sbuf(c, ph, "vsq", [128, 4, 512], BF16)
                        m2 = sbuf(c, ph, "m2", [128, 512], F32)
                        mean = sbuf(c, ph, "mean", [128, 512], F32)
                        var = sbuf(c, ph, "var", [128, 512], F32)
                        sdv = var
                        rsv = var
                        tt = [sbuf(c, ph, "tt", [128, 512], F32)] * 2
                        tu = [sbuf(c, ph, "tu", [128, 512], F32)] * 2
                        psc = [psum(c, ph, "psc", [128, 512], F32) for _ in range(2)]
                        psm = psum(c, ph, "psm", [128, 512], F32)
                        psq2 = psum(c, ph, "psq2", [128, 512], F32)
                        for cc in range(4):
                            for j in range(31):
                                eng = "dve"
                                col = 36 + cc * 31 + j
                                S.op(eng, lambda e, cc=cc, j=j, col=col: e.tensor_scalar(dg[:, cc, j, :], c.ident[:], sml[:, col:col + 1], None, ALU.mult),
                                     reads=["smt", "ident"], writes=[("dg", cc, j)])
                        S.barrier()
                        it = 0
                        for tg in range(4):
                            for cc in range(4):
                                pc = psc[it % 2]
                                for j in range(31):
                                    S.op("pe", lambda e, cc=cc, j=j, pc=pc: e.matmul(pc[:], dg[:, cc, j, :], zT[:, cc, 1 + tg * 512 + j:1 + tg * 512 + j + 512],
                                                                                  start=(j == 0), stop=(j == 30)),
                                         reads=[("zT", q) for q in range(6)], writes=[("psc", it % 2)], inc=(j == 30))
                                S.op("act", lambda e, cc=cc, pc=pc: e.activation(v32[:, cc, :], pc[:], AF.Identity, bias=sml[:, 24 + cc:25 + cc], scale=1.0),
                                     reads=[("psc", it % 2), "smt"], writes=[("v32", cc)])
                                S.op("dve", lambda e, cc=cc: e.tensor_copy(vb[:, cc, :], v32[:, cc, :]), reads=[("v32", cc)], writes=[("vb", cc)])
                                S.op("pool", lambda e, cc=cc: e.tensor_tensor(vsq[:, cc, :], v32[:, cc, :], v32[:, cc, :], ALU.mult), reads=[("v32", cc)], writes=[("vsq", cc)])
                                it += 1
                            for cc in range(4):
                                S.op("pe", lambda e, cc=cc: e.matmul(psm[:], c.onesC[:], vb[:, cc, :], start=(cc == 0), stop=(cc == 3)),
                                     reads=[("vb", cc)], writes=["psm"], inc=(cc == 3))
                            for cc in range(4):
                                S.op("pe", lambda e, cc=cc: e.matmul(psq2[:], c.onesC[:], vsq[:, cc, :], start=(cc == 0), stop=(cc == 3)),
                                     reads=[("vsq", cc)], writes=["psq2"], inc=(cc == 3))
                            S.op("act", lambda e: e.activation(mean[:], psm[:], AF.Copy), reads=["psm"], writes=["mean"])
                            S.op("act", lambda e: e.activation(m2[:], psm[:], AF.Square), reads=["psm"], writes=["m2"])
                            S.op("dve", lambda e: e.tensor_tensor(var[:], psq2[:], m2[:], ALU.subtract), reads=["psq2", "m2"], writes=["var", "sdv", "rsv"])
                            S.op("act", lambda e: e.activation(sdv[:], var[:], AF.Sqrt, bias=c.epsc[:, 0:1], scale=1.0), reads=["var", "rsv"], writes=["var", "sdv"])
                            S.op("dve", lambda e: e.reciprocal(rsv[:], sdv[:]), reads=["sdv"], writes=["var", "sdv", "rsv"])
                            for cc in range(4):
                                S.op("dve", lambda e, cc=cc: e.tensor_tensor(tt[cc % 2][:], v32[:, cc, :], mean[:], ALU.subtract), reads=[("v32", cc), "mean"], writes=[("tt", 0)])
                                S.op("pool", lambda e, cc=cc: e.tensor_tensor(tu[cc % 2][:], tt[cc % 2][:], rsv[:], ALU.mult), reads=[("tt", 0), "rsv"], writes=[("tu", 0)])
                                S.op("act", lambda e, cc=cc: e.activation(ybT[:, cc, tg * 512:(tg + 1) * 512], tu[cc % 2][:], AF.Silu,
                                                                        bias=sml[:, 32 + cc:33 + cc], scale=sml[:, 28 + cc:29 + cc]),
                                     reads=[("tu", 0), "smt"], writes=[("ybT", tg)])
                        S.barrier()
                        chk("Cb2")
                if dbg:
                    with ExitStack() as ph:
                        for nm, t in (("ya", yaT), ("yb", ybT), ("yc", ycT)):
                            if nm in dbgo:
                                for k in range(4):
                                    S.dma("pool", oslotp, lambda e, t=t, nm=nm, k=k: e.dma_start(out=dbgo[nm][k * 128:(k + 1) * 128, :], in_=t[:, k, :]), reads=[])
                        S.barrier()

                with ExitStack() as ph:
                    wr = mk_ring(c, ph, "wM", 4, 4096)
                    mk_h(ph, 512, 1)
                    wb = mk_ring(c, ph, "wMb", 4, 2048)
                    mg = sbuf(c, ph, "mg", [128, 8, 512], BF16)
                    sgt = [sbuf(c, ph, "sgt", [128, 512], F32) for _ in range(3)]
                    tt = [sbuf(c, ph, "ttm", [128, 512], F32) for _ in range(3)]
                    uu = sbuf(c, ph, "uu", [128, 512], F32)
                    c.ps_rms = psum(c, ph, "ps_rms", [128, 512], F32)
                    psg = [psum(c, ph, "psg", [128, 512], F32) for _ in range(3)]
                    psb = [psum(c, ph, "psb", [128, 512], F32) for _ in range(3)]
                    yTs = [yaT, ybT, ycT]
                    ykeys = [[("yaT", g) for g in range(4)], [("ybT", g) for g in range(4)], [("ycT", m) for m in range(16)]]
                    for g in range(4):
                        h, hk0 = next_h()
                        hk = rms_chunk(c, xT[:, :, g * 512:(g + 1) * 512], xkeys(g) + ["xT_all"], 512, g_mix, h, hk0)
                        for hf in range(2):
                            Wg3, Wb3 = [], []
                            for br in range(3):
                                Wg3.append(wget(c, wr, wview(w_in, 4096 + br * 1024 + hf * 512, 4096 + br * 1024 + hf * 512 + 512), 8, 512))
                                Wb3.append(wget(c, wb, wview(dr["w_br"][li][br], hf * 512, hf * 512 + 512), 4, 512))
                            for oc in range(4):
                                ocg = hf * 4 + oc
                                for br in range(3):
                                    proj_fm(c, psg[br][:], ("psg", br), Wg3[br][0], Wg3[br][1], oc * 128, h, hk, 512)
                                    Wb, wbk = Wb3[br]
                                    for k in range(4):
                                        S.op("pe", lambda e, k=k, br=br, Wb=Wb: e.matmul(psb[br][:], Wb[:, k, oc * 128:(oc + 1) * 128], yTs[br][:, k, g * 512:(g + 1) * 512],
                                                                                       start=(k == 0), stop=(k == 3)),
                                             reads=[wbk] + ykeys[br], writes=[("psb", br)], inc=(k == 3))
                                    S.op("act", lambda e, br=br: e.activation(sgt[br][:], psg[br][:], AF.Sigmoid), reads=[("psg", br)], writes=[("sgt", br)])
                                    S.op("dve", lambda e, br=br: e.tensor_tensor(tt[br][:], psb[br][:], sgt[br][:], ALU.mult), reads=[("psb", br), ("sgt", br)], writes=[("ttm", br)])
                                S.op("dve", lambda e: e.tensor_tensor(uu[:], tt[0][:], tt[1][:], ALU.add), reads=[("ttm", 0), ("ttm", 1)], writes=["uu"])
                                S.op("dve", lambda e, ocg=ocg: e.tensor_tensor(mg[:, ocg, :], uu[:], tt[2][:], ALU.add), reads=["uu", ("ttm", 2)], writes=[("mg", ocg)])
                        for hf in range(2):
                            Wo, wok = wget(c, wr, wview(dr["w_o"][li], hf * 512, hf * 512 + 512), 8, 512)
                            for oc in range(4):
                                ocg = hf * 4 + oc
                                po = psg[oc % 3]
                                proj_fm(c, po[:], ("psg", oc % 3), Wo, wok, oc * 128, mg, [("mg", q) for q in range(8)], 512)
                                S.op("dve", lambda e, ocg=ocg, po=po: e.tensor_tensor(xT[:, ocg, g * 512:(g + 1) * 512], xT[:, ocg, g * 512:(g + 1) * 512], po[:], ALU.add),
                                     reads=[("psg", oc % 3), ("x", ocg, g), "xT_all"], writes=[("x", ocg, g)])
                    S.barrier()
                    chk("M")
            if dbg and "xm" in dbgo:
                for k in range(4):
                    S.dma("sp", oslot, lambda e, k=k: e.dma_start(out=dbgo["xm"][k * 128:(k + 1) * 128, :], in_=xT[:, k, :]), reads=[])
                S.barrier()

            with ExitStack() as ph:
                wr = mk_ring(c, ph, "wF", 4, 4096)
                mk_h(ph, 512, 2)
                wd = mk_ring(c, ph, "wFd", 2, 22 * 256)
                wp = mk_ring(c, ph, "wP", 2, 4096)
                wpi = mk_ring(c, ph, "wPi", 1, 2048)
                pr = mk_ring(c, ph, "pTb", 2, 1024)
                actT = sbuf(c, ph, "actT", [128, 22, 512], BF16)
                sil = [sbuf(c, ph, "sil", [128, 512], F32) for _ in range(2)]
                tp = [sbuf(c, ph, "tp", [128, 512], F32) for _ in range(2)]
                c.ps_rms = psum(c, ph, "ps_rms", [128, 512], F32)
                psg = [psum(c, ph, "psg", [128, 512], F32) for _ in range(2)]
                psu = [psum(c, ph, "psu", [128, 512], F32) for _ in range(2)]
                psd = [psum(c, ph, "psd", [128, 512], F32) for _ in range(2)]
                Wpg = [wget(c, wp, wview(dr["w_pg"][li], hf * 512, hf * 512 + 512), 8, 512) for hf in range(2)]
                Wpi, wpik = wget(c, wpi, wview(dr["w_pi"][li], 0, 1024), 2, 1024)
                it = 0
                for g in range(4):
                    xs_ = xT[:, :, g * 512:(g + 1) * 512]
                    h, hk0 = next_h()
                    hk = rms_chunk(c, xs_, xkeys(g) + ["xT_all"], 512, g_ffn, h, hk0)
                    for t in range(6):
                        ncol = 512 if t < 5 else 256
                        Wg, wgk = wget(c, wr, wview(dr["w_fg"][li], t * 512, t * 512 + ncol), 8, ncol)
                        Wu, wuk = wget(c, wr, wview(dr["w_fu"][li], t * 512, t * 512 + ncol), 8, ncol)
                        for f4 in range(ncol // 128):
                            f = t * 4 + f4
                            pg, pu = psg[it % 2], psu[it % 2]
                            proj_fm(c, pg[:], ("psg", it % 2), Wg, wgk, f4 * 128, h, hk, 512)
                            proj_fm(c, pu[:], ("psu", it % 2), Wu, wuk, f4 * 128, h, hk, 512)
                            sl = sil[it % 2]
                            S.op("act", lambda e, sl=sl, pg=pg: e.activation(sl[:], pg[:], AF.Silu), reads=[("psg", it % 2)], writes=[("sil", it % 2)])
                            S.op("dve", lambda e, sl=sl, pu=pu, f=f: e.tensor_tensor(actT[:, f, :], pu[:], sl[:], ALU.mult),
                                 reads=[("psu", it % 2), ("sil", it % 2)], writes=[("actT", f)])
                            it += 1
                    for oq in range(4):
                        Wd, wdk = wget(c, wd, dr["w_fd"][li][:, oq * 256:(oq + 1) * 256].rearrange("(f p) n -> p f n", p=128), 22, 256)
                        for o2 in range(2):
                            oc = oq * 2 + o2
                            pd = psd[oc % 2]
                            for f in range(22):
                                S.op("pe", lambda e, f=f, Wd=Wd, pd=pd: e.matmul(pd[:], Wd[:, f, o2 * 128:(o2 + 1) * 128], actT[:, f, :], start=(f == 0), stop=(f == 21)),
                                     reads=[wdk, ("actT", f)], writes=[("psd", oc % 2)], inc=(f == 21))
                            S.op("dve", lambda e, oc=oc, pd=pd: e.tensor_tensor(xT[:, oc, g * 512:(g + 1) * 512], xT[:, oc, g * 512:(g + 1) * 512], pd[:], ALU.add),
                                 reads=[("psd", oc % 2), ("x", oc, g), "xT_all"], writes=[("x", oc, g)])
                    h, hk0 = next_h()
                    hk = rms_chunk(c, xs_, xkeys(g) + ["xT_all"], 512, g_ple, h, hk0)
                    pTb, pkey = wget(c, pr, dr["pT"][li][:, g * 512:(g + 1) * 512].rearrange("(k p) t -> p k t", p=128), 2, 512)
                    for oc in range(8):
                        pg, pu = psg[it % 2], psu[it % 2]
                        Wp, wpk = Wpg[oc // 4]
                        proj_fm(c, pg[:], ("psg", it % 2), Wp, wpk, (oc % 4) * 128, h, hk, 512)
                        for k in range(2):
                            S.op("pe", lambda e, k=k, pu=pu: e.matmul(pu[:], Wpi[:, k, oc * 128:(oc + 1) * 128], pTb[:, k, :], start=(k == 0), stop=(k == 1)),
                                 reads=[wpik, pkey], writes=[("psu", it % 2)], inc=(k == 1))
                        sl = sil[it % 2]
                        tpi = tp[it % 2]
                        S.op("act", lambda e, sl=sl, pg=pg: e.activation(sl[:], pg[:], AF.Sigmoid), reads=[("psg", it % 2)], writes=[("sil", it % 2)])
                        S.op("dve", lambda e, sl=sl, pu=pu, tpi=tpi: e.tensor_tensor(tpi[:], pu[:], sl[:], ALU.mult), reads=[("psu", it % 2), ("sil", it % 2)], writes=[("tp", it % 2)])
                        S.op("dve", lambda e, oc=oc, tpi=tpi: e.tensor_tensor(xT[:, oc, g * 512:(g + 1) * 512], xT[:, oc, g * 512:(g + 1) * 512], tpi[:], ALU.add),
                             reads=[("tp", it % 2), ("x", oc, g), "xT_all"], writes=[("x", oc, g)])
                        it += 1
                S.barrier()
                chk("FP")

            if exchange and li + 1 < nl:
                for q in range(4):
                    for k in range(8):
                        S.dma("sp", oslot, lambda e, k=k, q=q: e.dma_start(out=xs_d[q][k * 128:(k + 1) * 128, :], in_=xT[:, k, q * 512:(q + 1) * 512]),
                              reads=xkeys(q), writes=[("xs_d", q)])
                S.barrier()
                ccslot = S.slot()
                for q in range(4):
                    S.dma("pool", ccslot, lambda e, q=q: e.collective_compute("AllGather", ALU.bypass, replica_groups=[[0, 1], [2, 3], [4, 5], [6, 7]],
                                                                             ins=[xs_d[q]], outs=[xg_d[q]]), reads=[("xs_d", q)], writes=[("xg_d", q)], amt=1)
                S.barrier()
          except _Stop:
            break
        S.dead = False
        S.limit = None
        S.barrier()

        with ExitStack() as ph:
            if final:
                c.ps_rms = psum(c, ph, "ps_rms", [128, 512], F32)
                fin = [sbuf(c, ph, "fin", [128, 8, 512], F32) for _ in range(2)]
                sqf = sbuf(c, ph, "sqf", [128, 8, 512], BF16)
            for g in range(4):
                if final:
                    f = fin[g % 2]
                    fk = rms_chunk(c, xT[:, :, g * 512:(g + 1) * 512], xkeys(g) + ["xT_all"], 512, sm2[:, 0:8], f, ("fin", g % 2), sq=sqf)
                    S.dma("sp", oslot, lambda e, f=f: e.dma_start(out=yT[:, g * 512:(g + 1) * 512].rearrange("(k p) t -> p k t", p=128), in_=f[:]),
                          reads=fk, writes=["yT"])
                else:
                    S.dma("sp", oslot, lambda e: e.dma_start(out=yT[:, g * 512:(g + 1) * 512].rearrange("(k p) t -> p k t", p=128), in_=xT[:, :, g * 512:(g + 1) * 512]),
                          reads=xkeys(g), writes=["yT"])
            S.barrier()
    return nc


def _nbr_tables(rpb_l, half):
    specs = [(0, 0, 6), (1, 1, 5), (7, 7, 5), (14, 14, 5), (15, 14, 6)]
    out = []
    p = np.arange(128)
    q = np.arange(128)
    for m, rp0, nch in specs:
        for ch in range(nch):
            rp = rp0 + ch
            brow = 2 * rp + p // 64
            kcol = p % 64
            lrow = 2 * m + q // 64
            w = q % 64
            r = 32 * half + lrow
            kr = 32 * half - 4 + brow
            rstart = np.clip(r - 4, 0, 56)
            cs = np.clip(w - 8, 0, 48)
            KR, RR = kr[:, None], r[None, :]
            valid = (KR >= 0) & (KR <= 63) & (KR >= rstart[None, :]) & (KR < rstart[None, :] + 8) & \
                    (kcol[:, None] >= cs[None, :]) & (kcol[:, None] < cs[None, :] + 16)
            ro = np.clip(KR - RR + 7, 0, 14)
            co = np.clip(kcol[:, None] - w[None, :] + 15, 0, 30)
            vals = rpb_l[:, ro, co]
            out.append(np.where(valid[None], vals, np.float32(NEG)).astype(np.float32))
    t = np.stack(out, axis=2)
    return np.ascontiguousarray(t.reshape(8, 128, NTAB * 128))


def _consts():
    ident = np.eye(128, dtype=np.float32)
    rot = np.zeros((128, 128), np.float32)
    for cc in range(2):
        for d in range(64):
            if d < 32:
                rot[cc * 64 + d + 32, cc * 64 + d] = -1.0
            else:
                rot[cc * 64 + d - 32, cc * 64 + d] = 1.0
    inv = (1.0 / (np.float32(10000.0) ** (np.arange(0, 64, 2, dtype=np.float32) / np.float32(64)))).astype(np.float32)
    ang = np.arange(4096, dtype=np.float32)[:, None] * inv[None, :]
    ang = np.concatenate([ang, ang], axis=-1)
    cos = np.cos(ang).astype(np.float32).T
    sin = np.sin(ang).astype(np.float32).T
    cosg = np.ascontiguousarray(np.concatenate([cos, cos], axis=0))
    sing = np.ascontiguousarray(np.concatenate([sin, sin], axis=0))
    return np.stack([ident, rot]), cosg, sing


def _smalls(inp, l):
    def fm(v, k):
        return np.asarray(v, np.float32).reshape(k, 128).T

    cw = np.asarray(inp["conv_w"][l], np.float32).reshape(31, 4, 128).transpose(2, 1, 0).reshape(128, 124)
    parts = [fm(inp["norm_mix"][l], 8), fm(inp["norm_ffn"][l], 8), fm(inp["norm_ple"][l], 8),
             fm(inp["conv_b"][l], 4), fm(inp["cln_g"][l], 4), fm(inp["cln_b"][l], 4), cw,
             np.broadcast_to(np.asarray(inp["subln_g"][l], np.float32)[None, :], (128, 128)),
             np.broadcast_to(np.asarray(inp["lam_q1"][l], np.float32)[None, :], (128, 64)),
             np.broadcast_to(np.asarray(inp["lam_k1"][l], np.float32)[None, :], (128, 64)),
             np.broadcast_to(np.asarray(inp["lam_q2"][l], np.float32)[None, :], (128, 64)),
             np.broadcast_to(np.asarray(inp["lam_k2"][l], np.float32)[None, :], (128, 64)),
             np.asarray(inp["subln_g"][l], np.float32).reshape(128, 1)]
    sm = np.concatenate(parts, axis=1)
    assert sm.shape == (128, NSM)
    return np.ascontiguousarray(sm, dtype=np.float32)


_NC_CACHE = {}


def _get_nc(layer_ids, final, exchange, dbg=None):
    key = (tuple(layer_ids), final, exchange, tuple(dbg) if dbg else None)
    if key not in _NC_CACHE:
        _NC_CACHE[key] = build(list(layer_ids), final, exchange, dbg)
    return _NC_CACHE[key]


def _in_maps(inp, layer_ids, xown, xg):
    ctab, cosg, sing = _consts()
    ls = list(layer_ids)
    f32 = lambda a: np.ascontiguousarray(np.asarray(a, np.float32))
    shared = {
        "w_in": f32(np.asarray(inp["w_in"])[ls]),
        "w_br": f32(np.stack([np.asarray(inp["w_br_a"])[ls], np.asarray(inp["w_br_b"])[ls], np.asarray(inp["w_br_c"])[ls]], axis=1)),
        "w_o": f32(np.asarray(inp["w_o"])[ls]),
        "w_fg": f32(np.asarray(inp["w_ffn_gate"])[ls]),
        "w_fu": f32(np.asarray(inp["w_ffn_up"])[ls]),
        "w_fd": f32(np.asarray(inp["w_ffn_down"])[ls]),
        "w_pi": f32(np.asarray(inp["w_ple_in"])[ls]),
        "w_pg": f32(np.asarray(inp["w_ple_gate"])[ls]),
        "sm": f32(np.stack([_smalls(inp, l) for l in ls])),
        "cosg": cosg, "sing": sing, "ctab": f32(ctab),
    }
    p = np.asarray(inp["p"], np.float32)
    rpb = np.asarray(inp["rpb"], np.float32)
    nbts = [f32(np.stack([_nbr_tables(rpb[l], half) for l in ls])) for half in range(2)]
    nfin = np.asarray(inp["norm_final"], np.float32).reshape(8, 128).T
    maps = []
    for core in range(8):
        b, half = core // 2, core % 2
        sm2 = np.zeros((128, 16), np.float32)
        sm2[:, 0:8] = nfin
        sm2[:, 8] = 1.0 if half == 1 else 0.0
        sm2[:, 9] = 1.0 if half == 0 else 0.0
        m = dict(shared)
        m["xown"] = f32(xown[core])
        m["xg"] = f32(xg[b])
        m["pT"] = f32(p[ls][:, b, half * 2048:(half + 1) * 2048, :].transpose(0, 2, 1))
        m["sm2"] = sm2
        m["coso"] = f32(cosg[:, half * 2048:(half + 1) * 2048])
        m["sino"] = f32(sing[:, half * 2048:(half + 1) * 2048])
        m["nbt"] = nbts[half]
        maps.append(m)
    return maps


FUSED = True


def kernel(**inp):
    x = np.asarray(inp["x"], np.float32)
    xT = [np.ascontiguousarray(x[c // 2, (c % 2) * 2048:(c % 2 + 1) * 2048, :].T) for c in range(8)]
    xg = [np.stack([xT[2 * b], xT[2 * b + 1]]) for b in range(4)]
    if FUSED:
        nc = _get_nc((0, 1), True, True)
        res = run_bass_kernel_spmd(nc, _in_maps(inp, (0, 1), xT, xg), core_ids=list(range(8)))
        ys = [r["yT"] for r in res.results]
    else:
        nc0 = _get_nc((0,), False, False)
        res = run_bass_kernel_spmd(nc0, _in_maps(inp, (0,), xT, xg), core_ids=list(range(8)))
        xT = [np.asarray(r["yT"], np.float32) for r in res.results]
        xg = [np.stack([xT[2 * b], xT[2 * b + 1]]) for b in range(4)]
        nc1 = _get_nc((1,), True, False)
        res = run_bass_kernel_spmd(nc1, _in_maps(inp, (1,), xT, xg), core_ids=list(range(8)))
        ys = [r["yT"] for r in res.results]
    out = np.empty((4, 4096, 1024), np.float32)
    for c in range(8):
        out[c // 2, (c % 2) * 2048:(c % 2 + 1) * 2048, :] = np.asarray(ys[c], np.float32).T
    return out
```

```python
import math
import numpy as np
from contextlib import ExitStack
import concourse.bass as bass
import concourse.mybir as mybir
from concourse.bass_utils import run_bass_kernel_spmd

F32 = mybir.dt.float32
BF16 = mybir.dt.bfloat16
AF = mybir.ActivationFunctionType
ALU = mybir.AluOpType
AX = mybir.AxisListType

EPS = 1e-6
NEG = -30000.0
ENGS = ("pe", "act", "dve", "pool", "sp")
NSM = 545
NTAB = 27


class Slot:
    def __init__(self, sem):
        self.sem = sem
        self.count = 0


class Sched:
    def __init__(self, nc, stack):
        self.nc = nc
        self.eobj = {"pe": nc.tensor, "act": nc.scalar, "dve": nc.vector, "pool": nc.gpsimd, "sp": nc.sync}
        self.stack = stack
        self.cnt = {e: 0 for e in ENGS}
        self.esem = {e: stack.enter_context(nc.semaphore("s_" + e)) for e in ENGS}
        self.known = {e: {} for e in ENGS}
        self.res = {}
        self.slots = []
        self.semh = {}
        for e in ENGS:
            self.semh[self.esem[e].num] = self.esem[e]

    def slot(self):
        s = self.stack.enter_context(self.nc.semaphore("dq%d" % len(self.slots)))
        self.semh[s.num] = s
        sl = Slot(s)
        self.slots.append(sl)
        return sl

    def _r(self, key):
        r = self.res.get(key)
        if r is None:
            r = {"w": None, "r": {}}
            self.res[key] = r
        return r

    def _deps(self, eng, reads, writes):
        waits = {}
        own = self.esem[eng].num

        def add(t, same_ok):
            if t is None:
                return
            s, v = t
            if s == own and not same_ok:
                return
            if v > waits.get(s, 0):
                waits[s] = v

        for k in reads:
            add(self._r(k)["w"], True)
        for k in writes:
            r = self._r(k)
            add(r["w"], eng != "pe")
            for s, v in r["r"].items():
                add((s, v), eng != "pe")
        kn = self.known[eng]
        out = []
        for s, v in waits.items():
            if kn.get(s, 0) < v:
                kn[s] = v
                out.append((s, v))
        return out

    def _commit(self, tick, reads, writes):
        s, v = tick
        for k in reads:
            r = self._r(k)
            if r["r"].get(s, 0) < v:
                r["r"][s] = v
        for k in writes:
            r = self._r(k)
            r["w"] = tick
            r["r"] = {}

    dead = False
    nops = 0
    limit = None

    def _lim(self):
        self.nops += 1
        if self.limit is not None and self.nops > self.limit:
            self.dead = True

    @staticmethod
    def _excl(key):
        name = key if isinstance(key, str) else key[0]
        return isinstance(name, str) and (name.startswith("ps") or name in ("acc", "accn", "S", "Sn"))

    def _split(self, reads, writes):
        ex = [k for k in reads if self._excl(k)]
        if not ex:
            return reads, writes
        return [k for k in reads if not self._excl(k)], list(writes) + ex

    def op(self, eng, fn, reads=(), writes=(), inc=True):
        self._lim()
        if self.dead:
            return
        reads, writes = self._split(reads, writes)
        waits = self._deps(eng, reads, writes)
        if inc:
            self.cnt[eng] += 1
            tick = (self.esem[eng].num, self.cnt[eng])
        else:
            tick = (self.esem[eng].num, self.cnt[eng] + 1)
        self._commit(tick, reads, writes)
        engine = self.eobj[eng]
        for s, v in waits:
            engine.wait_ge(self.semh[s], v)
        ins = fn(engine)
        if inc:
            ins.then_inc(self.esem[eng], 1)

    def dma(self, eng, slot, fn, reads=(), writes=(), amt=16):
        self._lim()
        if self.dead:
            return
        waits = self._deps(eng, reads, writes)
        slot.count += amt
        tick = (slot.sem.num, slot.count)
        self._commit(tick, reads, writes)
        engine = self.eobj[eng]
        for s, v in waits:
            engine.wait_ge(self.semh[s], v)
        fn(engine).then_inc(slot.sem, amt)

    def barrier(self):
        if self.dead:
            return
        for e in ENGS:
            engine = self.eobj[e]
            kn = self.known[e]
            for f in ENGS:
                if f == e:
                    continue
                s = self.esem[f]
                if kn.get(s.num, 0) < self.cnt[f]:
                    kn[s.num] = self.cnt[f]
                    engine.wait_ge(s, self.cnt[f])
            for sl in self.slots:
                if kn.get(sl.sem.num, 0) < sl.count:
                    kn[sl.sem.num] = sl.count
                    engine.wait_ge(sl.sem, sl.count)


class Buf:
    def __init__(self, t, key, slot=None):
        self.t = t
        self.key = key
        self.slot = slot


class C:
    pass


class _Stop(Exception):
    pass


def sbuf(c, ph, name, shape, dt):
    c.uid += 1
    return ph.enter_context(c.nc.sbuf_tensor("%s_%d" % (name, c.uid), shape, dt))


def psum(c, ph, name, shape, dt):
    c.uid += 1
    return ph.enter_context(c.nc.psum_tensor("%s_%d" % (name, c.uid), shape, dt))


def mk_ring(c, ph, name, n, cols):
    return {"bufs": [Buf(sbuf(c, ph, name, [128, cols], BF16), (name, c.uid, i), c.S.slot()) for i in range(n)], "i": 0}


def wget(c, ring, src3d, k, n, eng="pool"):
    b = ring["bufs"][ring["i"] % len(ring["bufs"])]
    ring["i"] += 1
    view = b.t[:, 0:k * n].rearrange("p (k n) -> p k n", n=n)
    c.S.dma(eng, b.slot, lambda e: e.dma_start(out=view, in_=src3d), writes=[b.key])
    return view, b.key


def wview(w2d, c0, c1):
    return w2d[:, c0:c1].rearrange("(k p) n -> p k n", p=128)


def rms_chunk(c, xsrc, xkeys, n, gain, h, hkey, sq=None):
    S = c.S
    keys = [(hkey, k) for k in range(8)]
    if sq is None:
        sq, sqkeys = h, keys
    else:
        sqkeys = ["sqx"]
    S.op("act", lambda e: e.activation(sq[:, :, 0:n], xsrc, AF.Square), reads=xkeys, writes=sqkeys)
    psm = c.ps_rms[:, 0:n]
    for k in range(8):
        S.op("pe", lambda e, k=k: e.matmul(psm, c.onesD[:], sq[:, k, 0:n], start=(k == 0), stop=(k == 7)),
             reads=sqkeys, writes=["ps_rms"], inc=(k == 7))
    S.op("act", lambda e: e.activation(c.sd[:, 0:n], psm, AF.Ln, bias=c.epsc[:, 0:1], scale=1.0), reads=["ps_rms"], writes=["sd"])
    S.op("act", lambda e: e.activation(c.rstd[:, 0:n], c.sd[:, 0:n], AF.Exp, scale=-0.5), reads=["sd"], writes=["rstd"])
    for k in range(8):
        S.op("dve", lambda e, k=k: e.scalar_tensor_tensor(h[:, k, 0:n], xsrc[:, k, :], gain[:, k:k + 1], c.rstd[:, 0:n], ALU.mult, ALU.mult),
             reads=list(xkeys) + ["rstd"], writes=[(hkey, k)])
    return keys


def proj_fm(c, out_ps, pskey, W, wkey, col0, h, hkeys, n, nk=8, inc_last=True):
    for k in range(nk):
        c.S.op("pe", lambda e, k=k: e.matmul(out_ps, W[:, k, col0:col0 + 128], h[:, k, 0:n], start=(k == 0), stop=(k == nk - 1)),
               reads=[wkey] + list(hkeys), writes=[pskey], inc=(k == nk - 1))


def rope_evac(c, ps, pskey, psr, psrkey, cosb, sinb, ckey, out_ap, outkey, n, i):
    S = c.S
    ib, it_ = i % len(c.kb), i % len(c.t1)
    kb, t1, t2 = c.kb[ib], c.t1[it_], c.t2[it_]
    kk, k1, k2 = ("kb", ib), ("t1", it_), ("t2", it_)
    S.op("act", lambda e: e.activation(kb[:, 0:n], ps, AF.Copy), reads=[pskey], writes=[kk])
    S.op("pe", lambda e: e.matmul(psr, c.rot[:], kb[:, 0:n], start=True, stop=True), reads=[kk], writes=[psrkey])
    S.op("dve", lambda e: e.tensor_tensor(t1[:, 0:n], ps, cosb, ALU.mult), reads=[pskey, ckey], writes=[k1])
    S.op("dve", lambda e: e.tensor_tensor(t2[:, 0:n], psr, sinb, ALU.mult), reads=[psrkey, ckey], writes=[k2])
    S.op("pool", lambda e: e.tensor_tensor(out_ap, t1[:, 0:n], t2[:, 0:n], ALU.add), reads=[k1, k2], writes=[outkey])


def build(layer_ids, final, exchange, dbg=None, stop=None):
    nl = len(layer_ids)
    nc = bass.Bass("TRN2", target_bir_lowering=False)
    c = C()
    c.nc = nc
    c.uid = 0
    dr = {}

    def din(name, shape):
        dr[name] = nc.dram_tensor(name, shape, F32, kind="ExternalInput").ap()

    din("xown", [1024, 2048])
    din("xg", [2, 1024, 2048])
    din("pT", [nl, 256, 2048])
    din("w_in", [nl, 1024, 7168])
    din("w_br", [nl, 3, 512, 1024])
    din("w_o", [nl, 1024, 1024])
    din("w_fg", [nl, 1024, 2816])
    din("w_fu", [nl, 1024, 2816])
    din("w_fd", [nl, 2816, 1024])
    din("w_pi", [nl, 256, 1024])
    din("w_pg", [nl, 1024, 1024])
    din("sm", [nl, 128, NSM])
    din("sm2", [128, 16])
    din("cosg", [128, 4096])
    din("sing", [128, 4096])
    din("coso", [128, 2048])
    din("sino", [128, 2048])
    din("ctab", [2, 128, 128])
    din("nbt", [nl, 8, 128, NTAB * 128])
    yT = nc.dram_tensor("yT", [1024, 2048], F32, kind="ExternalOutput").ap()
    dbgo = {}
    if dbg:
        for nm in dbg:
            dbgo[nm] = nc.dram_tensor("dbg_" + nm, [512, 2048], F32, kind="ExternalOutput").ap()
    if exchange:
        xs_d = [nc.dram_tensor("xs_d%d" % q, [1024, 512], F32, kind="Internal").ap() for q in range(4)]
        xg_d = [nc.dram_tensor("xg_d%d" % q, [2 * 1024, 512], F32, kind="Internal").ap() for q in range(4)]

    with ExitStack() as top:
        S = Sched(nc, top)
        c.S = S
        if isinstance(stop, int):
            S.limit = stop
        xT = sbuf(c, top, "xT", [128, 8, 2048], F32)
        c.xT = xT
        smt = sbuf(c, top, "smt", [128, nl, NSM], F32)
        sm2 = sbuf(c, top, "sm2", [128, 16], F32)
        c.ident = sbuf(c, top, "ident", [128, 128], BF16)
        c.rot = sbuf(c, top, "rot", [128, 128], BF16)
        c.onesD = sbuf(c, top, "onesD", [128, 128], BF16)
        c.onesC = sbuf(c, top, "onesC", [128, 128], BF16)
        c.ones1 = sbuf(c, top, "ones1", [128, 128], BF16)
        c.onesL = sbuf(c, top, "onesL", [128, 128], BF16)
        c.epsc = sbuf(c, top, "epsc", [128, 1], F32)
        c.sd = sbuf(c, top, "sd", [128, 512], F32)
        c.rstd = sbuf(c, top, "rstd", [128, 512], F32)
        lamt = sbuf(c, top, "lamt", [128, 16], F32)
        gsc4 = sbuf(c, top, "gsc4", [128, 4, 128], F32)
        ld = S.slot()
        ldp = S.slot()
        xslot = S.slot()
        oslot = S.slot()
        oslotp = S.slot()

        for k in range(8):
            S.dma("sp", ld, lambda e, k=k: e.dma_start(out=xT[:, k, :], in_=dr["xown"][k * 128:(k + 1) * 128, :]), writes=["xT_all"])
        S.dma("sp", ld, lambda e: e.dma_start(out=smt[:], in_=dr["sm"].rearrange("l p n -> p l n")), writes=["smt"])
        S.dma("sp", ld, lambda e: e.dma_start(out=sm2[:], in_=dr["sm2"]), writes=["sm2"])
        S.dma("pool", ldp, lambda e: e.dma_start(out=c.ident[:], in_=dr["ctab"][0]), writes=["ident"])
        S.dma("pool", ldp, lambda e: e.dma_start(out=c.rot[:], in_=dr["ctab"][1]), writes=["rot"])
        S.op("dve", lambda e: e.memset(c.onesD[:], 1.0 / 1024.0), writes=["onesD"])
        S.op("dve", lambda e: e.memset(c.onesC[:], 1.0 / 512.0), writes=["onesC"])
        S.op("dve", lambda e: e.memset(c.ones1[:], 1.0), writes=["ones1"])
        S.op("dve", lambda e: e.memset(c.onesL[:], 1.0 / 128.0), writes=["onesL"])
        S.op("dve", lambda e: e.memset(c.epsc[:], EPS), writes=["epsc"])
        S.barrier()

        def xkeys(g):
            return [("x", k, g) for k in range(8)]

        hstate = {"bufs": [], "i": 0}

        def mk_h(ph, n, count):
            hstate["bufs"] = [sbuf(c, ph, "h", [128, 8, n], BF16) for _ in range(count)]
            hstate["i"] = 0

        def next_h():
            hstate["i"] += 1
            j = hstate["i"] % len(hstate["bufs"])
            return hstate["bufs"][j], ("h", j)

        def chk(name):
            import os as _os
            if _os.environ.get("KDEBUG"):
                print("phase", name, "nops", S.nops)
            if stop == name:
                S.dead = True

        for li, lid in enumerate(layer_ids):
          try:
            lam_init = 0.8 - 0.6 * math.exp(-0.3 * lid)
            sml = smt[:, li, :]
            g_mix = sml[:, 0:8]
            g_ffn = sml[:, 8:16]
            g_ple = sml[:, 16:24]
            w_in = dr["w_in"][li]
            xg = dr["xg"] if (li == 0 or not exchange) else None

            def xg_view(half, t0, n):
                if xg is not None:
                    return xg[half][:, t0:t0 + n].rearrange("(k p) t -> p k t", p=128)
                q, off = t0 // 512, t0 % 512
                assert off + n <= 512
                return xg_d[q][half * 1024:(half + 1) * 1024, off:off + n].rearrange("(k p) t -> p k t", p=128)

            S.op("dve", lambda e: e.tensor_tensor(gsc4[:, 0, 0:64], sml[:, 288:352], sml[:, 352:416], ALU.mult), reads=["smt"], writes=["lt0"])
            S.op("dve", lambda e: e.reduce_sum(lamt[:, 0:1], gsc4[:, 0, 0:64], AX.X), reads=["lt0"], writes=["lam0"])
            S.op("dve", lambda e: e.tensor_tensor(gsc4[:, 1, 0:64], sml[:, 416:480], sml[:, 480:544], ALU.mult), reads=["smt"], writes=["lt1"])
            S.op("dve", lambda e: e.reduce_sum(lamt[:, 1:2], gsc4[:, 1, 0:64], AX.X), reads=["lt1"], writes=["lam1"])
            S.op("act", lambda e: e.activation(lamt[:, 2:4], lamt[:, 0:2], AF.Exp), reads=["lam0", "lam1"], writes=["lam2"])
            S.op("dve", lambda e: e.tensor_tensor(lamt[:, 5:6], lamt[:, 3:4], lamt[:, 2:3], ALU.subtract), reads=["lam2"], writes=["lam3"])
            S.op("dve", lambda e: e.tensor_scalar(lamt[:, 4:5], lamt[:, 5:6], -lam_init, None, ALU.add), reads=["lam3"], writes=["nlam"])
            for q in range(4):
                S.op("dve", lambda e, q=q: e.tensor_scalar(gsc4[:, q, :], sml[:, 160:288], 1.0 - lam_init, None, ALU.mult),
                     reads=["smt", "lam0", "lam1"], writes=[("gsc4", q)])
            S.op("dve", lambda e: e.tensor_scalar(lamt[:, 6:7], sml[:, 544:545], 1.0 - lam_init, None, ALU.mult), reads=["smt"], writes=["gcol"])
            nlam = lamt[:, 4:5]
            gcol = lamt[:, 6:7]
            S.barrier()
            chk("init")

            with ExitStack() as Ly:
                yaT = sbuf(c, Ly, "yaT", [128, 4, 2048], BF16)
                with ExitStack() as A:
                    KT = sbuf(c, A, "KT", [128, 4, 4096], BF16)
                    V = sbuf(c, A, "V", [128, 32, 4, 129], BF16)
                    c.kb = [sbuf(c, A, "kb", [128, 512], BF16) for _ in range(2)]
                    c.t1 = [sbuf(c, A, "t1", [128, 512], F32)]
                    c.t2 = [sbuf(c, A, "t2", [128, 512], F32)]
                    cslot = [S.slot() for _ in range(2)]
                    with ExitStack() as ph:
                        wr = mk_ring(c, ph, "wA1", 2, 4096)
                        xs = sbuf(c, ph, "xs", [128, 8, 256], F32)
                        cosb = [sbuf(c, ph, "cosb", [128, 256], F32) for _ in range(2)]
                        sinb = [sbuf(c, ph, "sinb", [128, 256], F32) for _ in range(2)]
                        mk_h(ph, 256, 2)
                        c.ps_rms = psum(c, ph, "ps_rms", [128, 512], F32)
                        psk = [psum(c, ph, "psk", [128, 512], F32) for _ in range(2)]
                        psr = [psum(c, ph, "psr", [128, 512], F32) for _ in range(2)]
                        psv = [psum(c, ph, "psv", [128, 512], F32) for _ in range(2)]
                        Wk, wkk = wget(c, wr, wview(w_in, 512, 1024), 8, 512)
                        Wv, wvk = wget(c, wr, wview(w_in, 1024, 1536), 8, 512)
                        S.op("pool", lambda e: e.memset(V[:, :, :, 128:129], 1.0), writes=["Vones"])
                        it = 0
                        for ci in range(16):
                            half, t0 = ci // 8, (ci % 8) * 256
                            S.dma("sp", xslot, lambda e: e.dma_start(out=xs[:], in_=xg_view(half, t0, 256)), writes=["xs"])
                            cb = ci % 2
                            S.dma("sp", cslot[cb], lambda e: e.dma_start(out=cosb[cb][:], in_=dr["cosg"][:, ci * 256:(ci + 1) * 256]), writes=[("cos", cb)])
                            S.dma("sp", cslot[cb], lambda e: e.dma_start(out=sinb[cb][:], in_=dr["sing"][:, ci * 256:(ci + 1) * 256]), writes=[("cos", cb)])
                            h, hk0 = next_h()
                            hk = rms_chunk(c, xs[:], ["xs"], 256, g_mix, h, hk0)
                            for hd in range(4):
                                pk, pr = psk[it % 2], psr[it % 2]
                                proj_fm(c, pk[:, 0:256], ("psk", it % 2), Wk, wkk, hd * 128, h, hk, 256)
                                rope_evac(c, pk[:, 0:256], ("psk", it % 2), pr[:, 0:256], ("psr", it % 2), cosb[cb][:], sinb[cb][:], ("cos", cb),
                                          KT[:, hd, ci * 256:(ci + 1) * 256], ("KT", hd, ci // 2), 256, it)
                                it += 1
                            for j in range(2):
                                pv = psv[j % 2]
                                for k in range(8):
                                    S.op("pe", lambda e, k=k, j=j, pv=pv: e.matmul(pv[:], h[:, k, j * 128:(j + 1) * 128], Wv[:, k, :], start=(k == 0), stop=(k == 7)),
                                         reads=[wvk] + hk, writes=[("psv", j % 2)], inc=(k == 7))
                                kc = ci * 2 + j
                                S.op("act", lambda e, kc=kc, pv=pv: e.activation(V[:, kc, :, 0:128], pv[:].rearrange("p (h d) -> p h d", d=128), AF.Copy),
                                     reads=[("psv", j % 2)], writes=[("V", kc)])
                        S.barrier()
                        chk("A1")
                    with ExitStack() as ph:
                        wr = mk_ring(c, ph, "wA2", 1, 4096)
                        mk_h(ph, 512, 1)
                        cosq = sbuf(c, ph, "cosq", [128, 512], F32)
                        sinq = sbuf(c, ph, "sinq", [128, 512], F32)
                        Q = sbuf(c, ph, "QT", [128, 4, 512], BF16)
                        PT = [sbuf(c, ph, "PT", [128, 512], BF16) for _ in range(4)]
                        B = [sbuf(c, ph, "B", [128, 512], F32) for _ in range(4)]
                        accPb = [sbuf(c, ph, "accPb", [128, 512], BF16) for _ in range(2)]
                        sqb = sbuf(c, ph, "sqb", [128, 512], BF16)
                        accO = [psum(c, ph, "accO", [128, 512], F32) for _ in range(2)]
                        Sps = [[psum(c, ph, "Sps", [128, 512], F32) for _ in range(2)] for _ in range(2)]
                        c.ps_rms = psum(c, ph, "ps_misc", [128, 512], F32)
                        psR = [psum(c, ph, "psR", [128, 512], F32), c.ps_rms]
                        psRk = [("psR", 0), "ps_rms"]
                        Wq, wqk = wget(c, wr, wview(w_in, 0, 512), 8, 512)
                        it = 0
                        pti = 0
                        Bk = [("B", i) for i in range(4)]
                        for g in range(4):
                            S.dma("sp", cslot[0], lambda e: e.dma_start(out=cosq[:], in_=dr["coso"][:, g * 512:(g + 1) * 512]), writes=["cosq"])
                            S.dma("sp", cslot[0], lambda e: e.dma_start(out=sinq[:], in_=dr["sino"][:, g * 512:(g + 1) * 512]), writes=["cosq"])
                            h, hk0 = next_h()
                            hk = rms_chunk(c, xT[:, :, g * 512:(g + 1) * 512], xkeys(g) + ["xT_all"], 512, g_mix, h, hk0)
                            for hd in range(4):
                                proj_fm(c, c.ps_rms[:], "ps_rms", Wq, wqk, hd * 128, h, hk, 512)
                                rope_evac(c, c.ps_rms[:], "ps_rms", psR[0][:], ("psR", 0), cosq[:], sinq[:], "cosq",
                                          Q[:, hd, :], ("QT", hd), 512, it)
                                it += 1
                            for hd in range(4):
                                S.op("dve", lambda e: e.memset(B[0][:], 0.0), writes=[Bk[0]])

                                def qkpair(kc):
                                    for cc in range(2):
                                        sp = Sps[cc][kc % 2]
                                        S.op("pe", lambda e, cc=cc, sp=sp: e.matmul(sp[:], KT[cc * 64:(cc + 1) * 64, hd, kc * 128:(kc + 1) * 128],
                                                                                   Q[cc * 64:(cc + 1) * 64, hd, :], start=True, stop=True),
                                             reads=[("KT", hd, kc // 4), ("QT", hd)], writes=[("S", cc, kc % 2)])

                                def av(kc, cc, pidx):
                                    sp = Sps[cc][kc % 2]
                                    pt = PT[pidx % 4]
                                    pk_ = ("PT", pidx % 4)
                                    S.op("act", lambda e: e.activation(pt[:], sp[:], AF.Exp, scale=0.125), reads=[("S", cc, kc % 2)], writes=[pk_])
                                    S.op("pe", lambda e: e.matmul(accO[cc][:], V[:, kc, hd, 0:128], pt[:], start=(kc == 0), stop=(kc == 31)),
                                         reads=[pk_, ("V", kc)], writes=[("accO", cc)])
                                    if cc == 1:
                                        S.op("pe", lambda e: e.matmul(c.ps_rms[:], c.ones1[:], pt[:], start=(kc == 0), stop=(kc == 31)),
                                             reads=[pk_, "ones1"], writes=["ps_rms"])
                                    else:
                                        S.op("dve", lambda e: e.tensor_tensor(B[0][:], B[0][:], pt[:], ALU.add), reads=[pk_, Bk[0]], writes=[Bk[0]])

                                qkpair(0)
                                for kc in range(32):
                                    if kc + 1 < 32:
                                        qkpair(kc + 1)
                                    for cc in range(2):
                                        av(kc, cc, pti)
                                        pti += 1
                                S.op("dve", lambda e: e.tensor_copy(accPb[0][:], B[0][:]), reads=[Bk[0]], writes=[("accPb", 0)])
                                S.op("pe", lambda e: e.matmul(psR[0][:], c.ones1[:], accPb[0][:], start=True, stop=True), reads=[("accPb", 0), "ones1"], writes=[psRk[0]])
                                for cc in range(2):
                                    S.op("act", lambda e, cc=cc: e.activation(B[cc][:], psR[cc][:], AF.Ln), reads=[psRk[cc], ("accPb", 0)], writes=[Bk[cc]])
                                    S.op("act", lambda e, cc=cc: e.activation(B[cc][:], B[cc][:], AF.Exp, scale=-1.0), reads=[Bk[cc]], writes=[Bk[cc]])
                                S.op("dve", lambda e: e.tensor_tensor(B[2][:], accO[0][:], B[0][:], ALU.mult), reads=[("accO", 0), Bk[0], ("accPb", 0)], writes=[Bk[2]])
                                S.op("dve", lambda e: e.tensor_tensor(B[3][:], accO[1][:], B[1][:], ALU.mult), reads=[("accO", 1), Bk[1], ("accPb", 1)], writes=[Bk[3]])
                                S.op("dve", lambda e: e.scalar_tensor_tensor(B[0][:], B[3][:], nlam, B[2][:], ALU.mult, ALU.add), reads=[Bk[3], Bk[2], "nlam"], writes=[Bk[0]])
                                S.op("pool", lambda e: e.tensor_tensor(sqb[:], B[0][:], B[0][:], ALU.mult), reads=[Bk[0]], writes=["sqb"])
                                S.op("pe", lambda e: e.matmul(psR[0][:], c.onesL[:], sqb[:], start=True, stop=True), reads=["sqb", "onesL"], writes=[("psR", 0)])
                                S.op("act", lambda e: e.activation(B[1][:], psR[0][:], AF.Ln, bias=c.epsc[:, 0:1], scale=1.0), reads=[("psR", 0)], writes=[Bk[1]])
                                S.op("act", lambda e: e.activation(B[1][:], B[1][:], AF.Exp, scale=-0.5), reads=[Bk[1]], writes=[Bk[1]])
                                S.op("dve", lambda e: e.scalar_tensor_tensor(yaT[:, hd, g * 512:(g + 1) * 512], B[0][:], gcol, B[1][:], ALU.mult, ALU.mult),
                                     reads=[Bk[0], Bk[1], "gcol"], writes=[("yaT", g)])
                        S.barrier()

                ycT = sbuf(c, Ly, "ycT", [128, 4, 2048], BF16)
                with ExitStack() as Cn:
                    kcT = sbuf(c, Cn, "kcT", [128, 4, 2560], BF16)
                    Vc = sbuf(c, Cn, "Vc", [128, 20, 8, 65], BF16)
                    qcT = sbuf(c, Cn, "qcT", [128, 4, 2048], BF16)
                    with ExitStack() as ph:
                        wr = mk_ring(c, ph, "wC1", 3, 4096)
                        xs = sbuf(c, ph, "xs", [128, 8, 256], F32)
                        mk_h(ph, 256, 2)
                        c.ps_rms = psum(c, ph, "ps_rms", [128, 512], F32)
                        psq = [psum(c, ph, "psq", [128, 512], F32) for _ in range(3)]
                        psv = [psum(c, ph, "psv", [128, 512], F32) for _ in range(2)]
                        Wq, wqk = wget(c, wr, wview(w_in, 2560, 3072), 8, 512)
                        Wk, wkk = wget(c, wr, wview(w_in, 3072, 3584), 8, 512)
                        Wv, wvk = wget(c, wr, wview(w_in, 3584, 4096), 8, 512)
                        S.op("pool", lambda e: e.memset(Vc[:, :, :, 64:65], 1.0), writes=["Vcones"])
                        pi = 0
                        for ci in range(10):
                            h, hk0 = next_h()
                            own = ci < 8
                            if own:
                                hk = rms_chunk(c, xT[:, :, ci * 256:(ci + 1) * 256], xkeys(ci // 2) + ["xT_all"], 256, g_mix, h, hk0)
                                kcol, rpb_ = 256 + ci * 256, 2 + ci * 2
                            else:
                                if ci == 8:
                                    S.dma("sp", xslot, lambda e: e.dma_start(out=xs[:], in_=xg_view(0, 1792, 256)), writes=["xs"])
                                    kcol, rpb_ = 0, 0
                                else:
                                    S.dma("sp", xslot, lambda e: e.dma_start(out=xs[:], in_=xg_view(1, 0, 256)), writes=["xs"])
                                    kcol, rpb_ = 2304, 18
                                hk = rms_chunk(c, xs[:], ["xs"], 256, g_mix, h, hk0)
                            for hp in range(4):
                                if own:
                                    pq = psq[pi % 3]
                                    proj_fm(c, pq[:, 0:256], ("psq", pi % 3), Wq, wqk, hp * 128, h, hk, 256)
                                    S.op("act", lambda e, pq=pq, hp=hp: e.activation(qcT[:, hp, ci * 256:(ci + 1) * 256], pq[:, 0:256], AF.Copy),
                                         reads=[("psq", pi % 3)], writes=[("qcT", ci // 2)])
                                    pi += 1
                                pq = psq[pi % 3]
                                proj_fm(c, pq[:, 0:256], ("psq", pi % 3), Wk, wkk, hp * 128, h, hk, 256)
                                S.op("dve", lambda e, pq=pq, hp=hp: e.tensor_copy(kcT[:, hp, kcol:kcol + 256], pq[:, 0:256]),
                                     reads=[("psq", pi % 3)], writes=[("kcT", ci)])
                                pi += 1
                            for j in range(2):
                                pv = psv[j % 2]
                                for k in range(8):
                                    S.op("pe", lambda e, k=k, j=j, pv=pv: e.matmul(pv[:], h[:, k, j * 128:(j + 1) * 128], Wv[:, k, :], start=(k == 0), stop=(k == 7)),
                                         reads=[wvk] + hk, writes=[("psv", j % 2)], inc=(k == 7))
                                rp = rpb_ + j
                                S.op("act", lambda e, rp=rp, pv=pv: e.activation(Vc[:, rp, :, 0:64], pv[:].rearrange("p (h d) -> p h d", d=64), AF.Copy),
                                     reads=[("psv", j % 2)], writes=[("Vc", rp)])
                        S.barrier()
                        chk("Cn1")
                    with ExitStack() as ph:
                        tabs = [sbuf(c, ph, "tab", [128, NTAB, 128], F32) for _ in range(2)]
                        tslot = [S.slot() for _ in range(2)]
                        tmp = [sbuf(c, ph, "tmp", [128, 6, 128], F32) for _ in range(2)]
                        PTn = [sbuf(c, ph, "PTn", [128, 6, 128], BF16) for _ in range(2)]
                        rrn = [sbuf(c, ph, "rrn", [128, 1], F32) for _ in range(2)]
                        yct = [sbuf(c, ph, "yct", [128, 128], BF16) for _ in range(2)]
                        accn = psum(c, ph, "accn", [128, 2, 4, 128], F32)
                        Sn = [psum(c, ph, "Sn", [128, 8, 128], F32) for _ in range(2)]
                        psT = psum(c, ph, "psT", [128, 1024], BF16)
                        nbt = dr["nbt"][li]
                        items = [(hp, m, eh) for hp in range(4) for m in range(16) for eh in range(2)]

                        def geom(m):
                            if m == 0:
                                return 0, 6, 0
                            if m == 1:
                                return 1, 5, 6
                            if m == 14:
                                return 14, 5, 16
                            if m == 15:
                                return 14, 6, 21
                            return m, 5, 11

                        def stage1(it):
                            hp, m, eh = items[it]
                            if m == 0 and eh == 0:
                                for e2 in range(2):
                                    S.dma("sp", tslot[e2], lambda e, e2=e2: e.dma_start(out=tabs[e2][:], in_=nbt[2 * hp + e2].rearrange("p (t q) -> p t q", q=128)),
                                          writes=[("tab", e2)])
                            rp0, nch, tb = geom(m)
                            tbuf, tkey = tabs[eh], ("tab", eh)
                            sn, skey = Sn[it % 2], ("Sn", it % 2)
                            for ch in range(nch):
                                rp = rp0 + ch
                                S.op("pe", lambda e, ch=ch, rp=rp: e.matmul(sn[:, ch, :], kcT[eh * 64:(eh + 1) * 64, hp, rp * 128:(rp + 1) * 128],
                                                                            qcT[eh * 64:(eh + 1) * 64, hp, m * 128:(m + 1) * 128], start=True, stop=True),
                                     reads=[("kcT", kk) for kk in range(10)] + [("qcT", m // 4)], writes=[skey], inc=(ch == nch - 1))
                            tm, mkey = tmp[it % 2], ("tmp", it % 2)
                            S.op("dve", lambda e: e.scalar_tensor_tensor(tm[:, 0:4, :], sn[:, 0:4, :], 0.125, tbuf[:, tb:tb + 4, :], ALU.mult, ALU.add),
                                 reads=[skey, tkey], writes=[mkey])
                            S.op("dve", lambda e: e.scalar_tensor_tensor(tm[:, 4:nch, :], sn[:, 4:nch, :], 0.125, tbuf[:, tb + 4:tb + nch, :], ALU.mult, ALU.add),
                                 reads=[skey, tkey], writes=[(mkey, 1)])
                            pn, pkey = PTn[it % 2], ("PTn", it % 2)
                            S.op("act", lambda e: e.activation(pn[:, 0:nch, :], tm[:, 0:nch, :], AF.Exp), reads=[mkey, (mkey, 1)], writes=[pkey])

                        def stage2(it):
                            hp, m, eh = items[it]
                            hd = 2 * hp + eh
                            rp0, nch, tb = geom(m)
                            pn, pkey = PTn[it % 2], ("PTn", it % 2)
                            ti = it // 2
                            yc_, ykey = yct[ti % 2], ("yct", ti % 2)
                            asl = accn[:, it % 2, (it // 2) % 4, 0:65]
                            akey = ("accn", it % 2)
                            for ch in range(nch):
                                rp = rp0 + ch
                                S.op("pe", lambda e, ch=ch, rp=rp: e.matmul(asl, pn[:, ch, :], Vc[:, rp, hd, :], start=(ch == 0), stop=(ch == nch - 1)),
                                     reads=[pkey, "Vcones", ("Vc", rp)], writes=[akey], inc=(ch == nch - 1))
                            rn = rrn[it % 2]
                            S.op("dve", lambda e: e.reciprocal(rn[:], accn[:, it % 2, (it // 2) % 4, 64:65]), reads=[akey], writes=[("rrn", it % 2)])
                            S.op("act", lambda e: e.activation(yc_[:, eh * 64:(eh + 1) * 64], accn[:, it % 2, (it // 2) % 4, 0:64], AF.Identity, scale=rn[:, 0:1]),
                                 reads=[akey, ("rrn", it % 2)], writes=[(ykey, eh)])
                            if eh == 1:
                                pcol = (ti % 4) * 128
                                S.op("pe", lambda e: e.transpose(psT[:, pcol:pcol + 128], yc_[:], c.ident[:]), reads=[(ykey, 0), (ykey, 1)], writes=["psT"])
                                S.op("dve", lambda e: e.tensor_copy(ycT[:, hp, m * 128:(m + 1) * 128], psT[:, pcol:pcol + 128]), reads=["psT"], writes=[("ycT", m)])

                        stage1(0)
                        for it in range(len(items)):
                            if it + 1 < len(items):
                                stage1(it + 1)
                            stage2(it)
                        S.barrier()
                        chk("Cn2")

                ybT = sbuf(c, Ly, "ybT", [128, 4, 2048], BF16)
                with ExitStack() as Cb:
                    zT = sbuf(c, Cb, "zT", [128, 4, 2080], BF16)
                    with ExitStack() as ph:
                        wr = mk_ring(c, ph, "wB1", 2, 4096)
                        mk_h(ph, 512, 2)
                        xh = sbuf(c, ph, "xh", [128, 8, 32], F32)
                        sg = [sbuf(c, ph, "sg", [128, 512], F32) for _ in range(2)]
                        tz = sbuf(c, ph, "tz", [128, 32], F32)
                        c.ps_rms = psum(c, ph, "ps_rms", [128, 512], F32)
                        psa = [psum(c, ph, "psa", [128, 512], F32) for _ in range(2)]
                        psg = [psum(c, ph, "psg", [128, 512], F32) for _ in range(2)]
                        Wa, wak = wget(c, wr, wview(w_in, 1536, 2048), 8, 512)
                        Wg, wgk = wget(c, wr, wview(w_in, 2048, 2560), 8, 512)
                        it = 0
                        for ci in range(5):
                            h, hk0 = next_h()
                            if ci < 4:
                                n = 512
                                hk = rms_chunk(c, xT[:, :, ci * 512:(ci + 1) * 512], xkeys(ci) + ["xT_all"], 512, g_mix, h, hk0)
                            else:
                                n = 32
                                S.dma("sp", xslot, lambda e: e.dma_start(out=xh[:, :, 0:16], in_=xg_view(0, 2032, 16)), writes=["xh"])
                                S.dma("sp", xslot, lambda e: e.dma_start(out=xh[:, :, 16:32], in_=xg_view(1, 0, 16)), writes=["xh"])
                                hk = rms_chunk(c, xh[:], ["xh"], 32, g_mix, h, hk0)
                            for cc in range(4):
                                pa, pg = psa[it % 2], psg[it % 2]
                                proj_fm(c, pa[:, 0:n], ("psa", it % 2), Wa, wak, cc * 128, h, hk, n)
                                proj_fm(c, pg[:, 0:n], ("psg", it % 2), Wg, wgk, cc * 128, h, hk, n)
                                sgi = sg[it % 2]
                                S.op("act", lambda e, pg=pg, sgi=sgi: e.activation(sgi[:, 0:n], pg[:, 0:n], AF.Sigmoid), reads=[("psg", it % 2)], writes=[("sg", it % 2)])
                                if ci < 4:
                                    S.op("dve", lambda e, pa=pa, sgi=sgi, cc=cc: e.tensor_tensor(zT[:, cc, 16 + ci * 512:16 + (ci + 1) * 512], pa[:, 0:n], sgi[:, 0:n], ALU.mult),
                                         reads=[("psa", it % 2), ("sg", it % 2)], writes=[("zT", ci)])
                                else:
                                    S.op("dve", lambda e, pa=pa, sgi=sgi: e.tensor_tensor(tz[:], pa[:, 0:32], sgi[:, 0:32], ALU.mult),
                                         reads=[("psa", it % 2), ("sg", it % 2)], writes=["tz"])
                                    S.op("dve", lambda e, cc=cc: e.tensor_scalar(zT[:, cc, 0:16], tz[:, 0:16], sm2[:, 8:9], None, ALU.mult), reads=["tz", "sm2"], writes=[("zT", 4)])
                                    S.op("dve", lambda e, cc=cc: e.tensor_scalar(zT[:, cc, 2064:2080], tz[:, 16:32], sm2[:, 9:10], None, ALU.mult), reads=["tz", "sm2"], writes=[("zT", 5)])
                                it += 1
                        S.barrier()
                        chk("Cb1")
                    with ExitStack() as ph:
                        dg = sbuf(c, ph, "dg", [128, 4, 31, 128], BF16)
                        v32 = sbuf(c, ph, "v32", [128, 4, 512], F32)
                        vb = sbuf(c, ph, "vb", [128, 4, 512], BF16)
                        vsq = sbuf(c, ph, "vsq", [128, 4, 512], BF16)
                        m2 = sbuf(c, ph, "m2", [128, 512], F32)
                        mean = sbuf(c, ph, "mean", [128, 512], F32)
                        var = sbuf(c, ph, "var", [128, 512], F32)
                        sdv = var
                        rsv = var
                        tt = [sbuf(c, ph, "tt", [128, 512], F32)] * 2
                        tu = [sbuf(c, ph, "tu", [128, 512], F32)] * 2
                        psc = [psum(c, ph, "psc", [128, 512], F32) for _ in range(2)]
                        psm = psum(c, ph, "psm", [128, 512], F32)
                        psq2 = psum(c, ph, "psq2", [128, 512], F32)
                        for cc in range(4):
                            for j in range(31):
                                eng = "dve"
                                col = 36 + cc * 31 + j
                                S.op(eng, lambda e, cc=cc, j=j, col=col: e.tensor_scalar(dg[:, cc, j, :], c.ident[:], sml[:, col:col + 1], None, ALU.mult),
                                     reads=["smt", "ident"], writes=[("dg", cc, j)])
                        S.barrier()
                        it = 0
                        for tg in range(4):
                            for cc in range(4):
                                pc = psc[it % 2]
                                for j in range(31):
                                    S.op("pe", lambda e, cc=cc, j=j, pc=pc: e.matmul(pc[:], dg[:, cc, j, :], zT[:, cc, 1 + tg * 512 + j:1 + tg * 512 + j + 512],
                                                                                  start=(j == 0), stop=(j == 30)),
                                         reads=[("zT", q) for q in range(6)], writes=[("psc", it % 2)], inc=(j == 30))
                                S.op("act", lambda e, cc=cc, pc=pc: e.activation(v32[:, cc, :], pc[:], AF.Identity, bias=sml[:, 24 + cc:25 + cc], scale=1.0),
                                     reads=[("psc", it % 2), "smt"], writes=[("v32", cc)])
                                S.op("dve", lambda e, cc=cc: e.tensor_copy(vb[:, cc, :], v32[:, cc, :]), reads=[("v32", cc)], writes=[("vb", cc)])
                                S.op("pool", lambda e, cc=cc: e.tensor_tensor(vsq[:, cc, :], v32[:, cc, :], v32[:, cc, :], ALU.mult), reads=[("v32", cc)], writes=[("vsq", cc)])
                                it += 1
                            for cc in range(4):
                                S.op("pe", lambda e, cc=cc: e.matmul(psm[:], c.onesC[:], vb[:, cc, :], start=(cc == 0), stop=(cc == 3)),
                                     reads=[("vb", cc)], writes=["psm"], inc=(cc == 3))
                            for cc in range(4):
                                S.op("pe", lambda e, cc=cc: e.matmul(psq2[:], c.onesC[:], vsq[:, cc, :], start=(cc == 0), stop=(cc == 3)),
                                     reads=[("vsq", cc)], writes=["psq2"], inc=(cc == 3))
                            S.op("act", lambda e: e.activation(mean[:], psm[:], AF.Copy), reads=["psm"], writes=["mean"])
                            S.op("act", lambda e: e.activation(m2[:], psm[:], AF.Square), reads=["psm"], writes=["m2"])
                            S.op("dve", lambda e: e.tensor_tensor(var[:], psq2[:], m2[:], ALU.subtract), reads=["psq2", "m2"], writes=["var", "sdv", "rsv"])
                            S.op("act", lambda e: e.activation(sdv[:], var[:], AF.Sqrt, bias=c.epsc[:, 0:1], scale=1.0), reads=["var", "rsv"], writes=["var", "sdv"])
                            S.op("dve", lambda e: e.reciprocal(rsv[:], sdv[:]), reads=["sdv"], writes=["var", "sdv", "rsv"])
                            for cc in range(4):
                                S.op("dve", lambda e, cc=cc: e.tensor_tensor(tt[cc % 2][:], v32[:, cc, :], mean[:], ALU.subtract), reads=[("v32", cc), "mean"], writes=[("tt", 0)])
                                S.op("pool", lambda e, cc=cc: e.tensor_tensor(tu[cc % 2][:], tt[cc % 2][:], rsv[:], ALU.mult), reads=[("tt", 0), "rsv"], writes=[("tu", 0)])
                                S.op("act", lambda e, cc=cc: e.activation(ybT[:, cc, tg * 512:(tg + 1) * 512], tu[cc % 2][:], AF.Silu,
                                                                        bias=sml[:, 32 + cc:33 + cc], scale=sml[:, 28 + cc:29 + cc]),
                                     reads=[("tu", 0), "smt"], writes=[("ybT", tg)])
                        S.barrier()
                        chk("Cb2")
                if dbg:
                    with ExitStack() as ph:
                        for nm, t in (("ya", yaT), ("yb", ybT), ("yc", ycT)):
                            if nm in dbgo:
                                for k in range(4):
                                    S.dma("pool", oslotp, lambda e, t=t, nm=nm, k=k: e.dma_start(out=dbgo[nm][k * 128:(k + 1) * 128, :], in_=t[:, k, :]), reads=[])
                        S.barrier()

                with ExitStack() as ph:
                    wr = mk_ring(c, ph, "wM", 4, 4096)
                    mk_h(ph, 512, 1)
                    wb = mk_ring(c, ph, "wMb", 4, 2048)
                    mg = sbuf(c, ph, "mg", [128, 8, 512], BF16)
                    sgt = [sbuf(c, ph, "sgt", [128, 512], F32) for _ in range(3)]
                    tt = [sbuf(c, ph, "ttm", [128, 512], F32) for _ in range(3)]
                    uu = sbuf(c, ph, "uu", [128, 512], F32)
                    c.ps_rms = psum(c, ph, "ps_rms", [128, 512], F32)
                    psg = [psum(c, ph, "psg", [128, 512], F32) for _ in range(3)]
                    psb = [psum(c, ph, "psb", [128, 512], F32) for _ in range(3)]
                    yTs = [yaT, ybT, ycT]
                    ykeys = [[("yaT", g) for g in range(4)], [("ybT", g) for g in range(4)], [("ycT", m) for m in range(16)]]
                    for g in range(4):
                        h, hk0 = next_h()
                        hk = rms_chunk(c, xT[:, :, g * 512:(g + 1) * 512], xkeys(g) + ["xT_all"], 512, g_mix, h, hk0)
                        for hf in range(2):
                            Wg3, Wb3 = [], []
                            for br in range(3):
                                Wg3.append(wget(c, wr, wview(w_in, 4096 + br * 1024 + hf * 512, 4096 + br * 1024 + hf * 512 + 512), 8, 512))
                                Wb3.append(wget(c, wb, wview(dr["w_br"][li][br], hf * 512, hf * 512 + 512), 4, 512))
                            for oc in range(4):
                                ocg = hf * 4 + oc
                                for br in range(3):
                                    proj_fm(c, psg[br][:], ("psg", br), Wg3[br][0], Wg3[br][1], oc * 128, h, hk, 512)
                                    Wb, wbk = Wb3[br]
                                    for k in range(4):
                                        S.op("pe", lambda e, k=k, br=br, Wb=Wb: e.matmul(psb[br][:], Wb[:, k, oc * 128:(oc + 1) * 128], yTs[br][:, k, g * 512:(g + 1) * 512],
                                                                                       start=(k == 0), stop=(k == 3)),
                                             reads=[wbk] + ykeys[br], writes=[("psb", br)], inc=(k == 3))
                                    S.op("act", lambda e, br=br: e.activation(sgt[br][:], psg[br][:], AF.Sigmoid), reads=[("psg", br)], writes=[("sgt", br)])
                                    S.op("dve", lambda e, br=br: e.tensor_tensor(tt[br][:], psb[br][:], sgt[br][:], ALU.mult), reads=[("psb", br), ("sgt", br)], writes=[("ttm", br)])
                                S.op("dve", lambda e: e.tensor_tensor(uu[:], tt[0][:], tt[1][:], ALU.add), reads=[("ttm", 0), ("ttm", 1)], writes=["uu"])
                                S.op("dve", lambda e, ocg=ocg: e.tensor_tensor(mg[:, ocg, :], uu[:], tt[2][:], ALU.add), reads=["uu", ("ttm", 2)], writes=[("mg", ocg)])
                        for hf in range(2):
                            Wo, wok = wget(c, wr, wview(dr["w_o"][li], hf * 512, hf * 512 + 512), 8, 512)
                            for oc in range(4):
                                ocg = hf * 4 + oc
                                po = psg[oc % 3]
                                proj_fm(c, po[:], ("psg", oc % 3), Wo, wok, oc * 128, mg, [("mg", q) for q in range(8)], 512)
                                S.op("dve", lambda e, ocg=ocg, po=po: e.tensor_tensor(xT[:, ocg, g * 512:(g + 1) * 512], xT[:, ocg, g * 512:(g + 1) * 512], po[:], ALU.add),
                                     reads=[("psg", oc % 3), ("x", ocg, g), "xT_all"], writes=[("x", ocg, g)])
                    S.barrier()
                    chk("M")
            if dbg and "xm" in dbgo:
                for k in range(4):
                    S.dma("sp", oslot, lambda e, k=k: e.dma_start(out=dbgo["xm"][k * 128:(k + 1) * 128, :], in_=xT[:, k, :]), reads=[])
                S.barrier()

            with ExitStack() as ph:
                wr = mk_ring(c, ph, "wF", 4, 4096)
                mk_h(ph, 512, 2)
                wd = mk_ring(c, ph, "wFd", 2, 22 * 256)
                wp = mk_ring(c, ph, "wP", 2, 4096)
                wpi = mk_ring(c, ph, "wPi", 1, 2048)
                pr = mk_ring(c, ph, "pTb", 2, 1024)
                actT = sbuf(c, ph, "actT", [128, 22, 512], BF16)
                sil = [sbuf(c, ph, "sil", [128, 512], F32) for _ in range(2)]
                tp = [sbuf(c, ph, "tp", [128, 512], F32) for _ in range(2)]
                c.ps_rms = psum(c, ph, "ps_rms", [128, 512], F32)
                psg = [psum(c, ph, "psg", [128, 512], F32) for _ in range(2)]
                psu = [psum(c, ph, "psu", [128, 512], F32) for _ in range(2)]
                psd = [psum(c, ph, "psd", [128, 512], F32) for _ in range(2)]
                Wpg = [wget(c, wp, wview(dr["w_pg"][li], hf * 512, hf * 512 + 512), 8, 512) for hf in range(2)]
                Wpi, wpik = wget(c, wpi, wview(dr["w_pi"][li], 0, 1024), 2, 1024)
                it = 0
                def rms_ffn(g):
                    h_, hk0_ = next_h()
                    return h_, rms_chunk(c, xT[:, :, g * 512:(g + 1) * 512], xkeys(g) + ["xT_all"], 512, g_ffn, h_, hk0_)

                hnext = rms_ffn(0)
                for g in range(4):
                    xs_ = xT[:, :, g * 512:(g + 1) * 512]
                    h, hk = hnext
                    for t in range(6):
                        ncol = 512 if t < 5 else 256
                        Wg, wgk = wget(c, wr, wview(dr["w_fg"][li], t * 512, t * 512 + ncol), 8, ncol)
                        Wu, wuk = wget(c, wr, wview(dr["w_fu"][li], t * 512, t * 512 + ncol), 8, ncol)
                        for f4 in range(ncol // 128):
                            f = t * 4 + f4
                            pg, pu = psg[it % 2], psu[it % 2]
                            proj_fm(c, pg[:], ("psg", it % 2), Wg, wgk, f4 * 128, h, hk, 512)
                            proj_fm(c, pu[:], ("psu", it % 2), Wu, wuk, f4 * 128, h, hk, 512)
                            sl = sil[it % 2]
                            S.op("act", lambda e, sl=sl, pg=pg: e.activation(sl[:], pg[:], AF.Silu), reads=[("psg", it % 2)], writes=[("sil", it % 2)])
                            S.op("dve", lambda e, sl=sl, pu=pu, f=f: e.tensor_tensor(actT[:, f, :], pu[:], sl[:], ALU.mult),
                                 reads=[("psu", it % 2), ("sil", it % 2)], writes=[("actT", f)])
                            it += 1
                    if g + 1 < 4:
                        hnext = rms_ffn(g + 1)
                    for oq in range(4):
                        Wd, wdk = wget(c, wd, dr["w_fd"][li][:, oq * 256:(oq + 1) * 256].rearrange("(f p) n -> p f n", p=128), 22, 256)
                        for o2 in range(2):
                            oc = oq * 2 + o2
                            pd = psd[oc % 2]
                            for f in range(22):
                                S.op("pe", lambda e, f=f, Wd=Wd, pd=pd: e.matmul(pd[:], Wd[:, f, o2 * 128:(o2 + 1) * 128], actT[:, f, :], start=(f == 0), stop=(f == 21)),
                                     reads=[wdk, ("actT", f)], writes=[("psd", oc % 2)], inc=(f == 21))
                            S.op("dve", lambda e, oc=oc, pd=pd: e.tensor_tensor(xT[:, oc, g * 512:(g + 1) * 512], xT[:, oc, g * 512:(g + 1) * 512], pd[:], ALU.add),
                                 reads=[("psd", oc % 2), ("x", oc, g), "xT_all"], writes=[("x", oc, g)])
                    h, hk0 = next_h()
                    hk = rms_chunk(c, xs_, xkeys(g) + ["xT_all"], 512, g_ple, h, hk0)
                    pTb, pkey = wget(c, pr, dr["pT"][li][:, g * 512:(g + 1) * 512].rearrange("(k p) t -> p k t", p=128), 2, 512)
                    for oc in range(8):
                        pg, pu = psg[it % 2], psu[it % 2]
                        Wp, wpk = Wpg[oc // 4]
                        proj_fm(c, pg[:], ("psg", it % 2), Wp, wpk, (oc % 4) * 128, h, hk, 512)
                        for k in range(2):
                            S.op("pe", lambda e, k=k, pu=pu: e.matmul(pu[:], Wpi[:, k, oc * 128:(oc + 1) * 128], pTb[:, k, :], start=(k == 0), stop=(k == 1)),
                                 reads=[wpik, pkey], writes=[("psu", it % 2)], inc=(k == 1))
                        sl = sil[it % 2]
                        tpi = tp[it % 2]
                        S.op("act", lambda e, sl=sl, pg=pg: e.activation(sl[:], pg[:], AF.Sigmoid), reads=[("psg", it % 2)], writes=[("sil", it % 2)])
                        S.op("dve", lambda e, sl=sl, pu=pu, tpi=tpi: e.tensor_tensor(tpi[:], pu[:], sl[:], ALU.mult), reads=[("psu", it % 2), ("sil", it % 2)], writes=[("tp", it % 2)])
                        S.op("dve", lambda e, oc=oc, tpi=tpi: e.tensor_tensor(xT[:, oc, g * 512:(g + 1) * 512], xT[:, oc, g * 512:(g + 1) * 512], tpi[:], ALU.add),
                             reads=[("tp", it % 2), ("x", oc, g), "xT_all"], writes=[("x", oc, g)])
                        it += 1
                S.barrier()
                chk("FP")

            if exchange and li + 1 < nl:
                for q in range(4):
                    for k in range(8):
                        S.dma("sp", oslot, lambda e, k=k, q=q: e.dma_start(out=xs_d[q][k * 128:(k + 1) * 128, :], in_=xT[:, k, q * 512:(q + 1) * 512]),
                              reads=xkeys(q), writes=[("xs_d", q)])
                S.barrier()
                ccslot = S.slot()
                for q in range(4):
                    S.dma("pool", ccslot, lambda e, q=q: e.collective_compute("AllGather", ALU.bypass, replica_groups=[[0, 1], [2, 3], [4, 5], [6, 7]],
                                                                             ins=[xs_d[q]], outs=[xg_d[q]]), reads=[("xs_d", q)], writes=[("xg_d", q)], amt=1)
                S.barrier()
          except _Stop:
            break
        S.dead = False
        S.limit = None
        S.barrier()

        with ExitStack() as ph:
            if final:
                c.ps_rms = psum(c, ph, "ps_rms", [128, 512], F32)
                fin = [sbuf(c, ph, "fin", [128, 8, 512], F32) for _ in range(2)]
                sqf = sbuf(c, ph, "sqf", [128, 8, 512], BF16)
            for g in range(4):
                if final:
                    f = fin[g % 2]
                    fk = rms_chunk(c, xT[:, :, g * 512:(g + 1) * 512], xkeys(g) + ["xT_all"], 512, sm2[:, 0:8], f, ("fin", g % 2), sq=sqf)
                    S.dma("sp", oslot, lambda e, f=f: e.dma_start(out=yT[:, g * 512:(g + 1) * 512].rearrange("(k p) t -> p k t", p=128), in_=f[:]),
                          reads=fk, writes=["yT"])
                else:
                    S.dma("sp", oslot, lambda e: e.dma_start(out=yT[:, g * 512:(g + 1) * 512].rearrange("(k p) t -> p k t", p=128), in_=xT[:, :, g * 512:(g + 1) * 512]),
                          reads=xkeys(g), writes=["yT"])
            S.barrier()
    return nc


def _nbr_tables(rpb_l, half):
    specs = [(0, 0, 6), (1, 1, 5), (7, 7, 5), (14, 14, 5), (15, 14, 6)]
    out = []
    p = np.arange(128)
    q = np.arange(128)
    for m, rp0, nch in specs:
        for ch in range(nch):
            rp = rp0 + ch
            brow = 2 * rp + p // 64
            kcol = p % 64
            lrow = 2 * m + q // 64
            w = q % 64
            r = 32 * half + lrow
            kr = 32 * half - 4 + brow
            rstart = np.clip(r - 4, 0, 56)
            cs = np.clip(w - 8, 0, 48)
            KR, RR = kr[:, None], r[None, :]
            valid = (KR >= 0) & (KR <= 63) & (KR >= rstart[None, :]) & (KR < rstart[None, :] + 8) & \
                    (kcol[:, None] >= cs[None, :]) & (kcol[:, None] < cs[None, :] + 16)
            ro = np.clip(KR - RR + 7, 0, 14)
            co = np.clip(kcol[:, None] - w[None, :] + 15, 0, 30)
            vals = rpb_l[:, ro, co]
            out.append(np.where(valid[None], vals, np.float32(NEG)).astype(np.float32))
    t = np.stack(out, axis=2)
    return np.ascontiguousarray(t.reshape(8, 128, NTAB * 128))


def _consts():
    ident = np.eye(128, dtype=np.float32)
    rot = np.zeros((128, 128), np.float32)
    for cc in range(2):
        for d in range(64):
            if d < 32:
                rot[cc * 64 + d + 32, cc * 64 + d] = -1.0
            else:
                rot[cc * 64 + d - 32, cc * 64 + d] = 1.0
    inv = (1.0 / (np.float32(10000.0) ** (np.arange(0, 64, 2, dtype=np.float32) / np.float32(64)))).astype(np.float32)
    ang = np.arange(4096, dtype=np.float32)[:, None] * inv[None, :]
    ang = np.concatenate([ang, ang], axis=-1)
    cos = np.cos(ang).astype(np.float32).T
    sin = np.sin(ang).astype(np.float32).T
    cosg = np.ascontiguousarray(np.concatenate([cos, cos], axis=0))
    sing = np.ascontiguousarray(np.concatenate([sin, sin], axis=0))
    return np.stack([ident, rot]), cosg, sing


def _smalls(inp, l):
    def fm(v, k):
        return np.asarray(v, np.float32).reshape(k, 128).T

    cw = np.asarray(inp["conv_w"][l], np.float32).reshape(31, 4, 128).transpose(2, 1, 0).reshape(128, 124)
    parts = [fm(inp["norm_mix"][l], 8), fm(inp["norm_ffn"][l], 8), fm(inp["norm_ple"][l], 8),
             fm(inp["conv_b"][l], 4), fm(inp["cln_g"][l], 4), fm(inp["cln_b"][l], 4), cw,
             np.broadcast_to(np.asarray(inp["subln_g"][l], np.float32)[None, :], (128, 128)),
             np.broadcast_to(np.asarray(inp["lam_q1"][l], np.float32)[None, :], (128, 64)),
             np.broadcast_to(np.asarray(inp["lam_k1"][l], np.float32)[None, :], (128, 64)),
             np.broadcast_to(np.asarray(inp["lam_q2"][l], np.float32)[None, :], (128, 64)),
             np.broadcast_to(np.asarray(inp["lam_k2"][l], np.float32)[None, :], (128, 64)),
             np.asarray(inp["subln_g"][l], np.float32).reshape(128, 1)]
    sm = np.concatenate(parts, axis=1)
    assert sm.shape == (128, NSM)
    return np.ascontiguousarray(sm, dtype=np.float32)


_NC_CACHE = {}


def _get_nc(layer_ids, final, exchange, dbg=None):
    key = (tuple(layer_ids), final, exchange, tuple(dbg) if dbg else None)
    if key not in _NC_CACHE:
        _NC_CACHE[key] = build(list(layer_ids), final, exchange, dbg)
    return _NC_CACHE[key]


def _in_maps(inp, layer_ids, xown, xg):
    ctab, cosg, sing = _consts()
    ls = list(layer_ids)
    f32 = lambda a: np.ascontiguousarray(np.asarray(a, np.float32))
    shared = {
        "w_in": f32(np.asarray(inp["w_in"])[ls]),
        "w_br": f32(np.stack([np.asarray(inp["w_br_a"])[ls], np.asarray(inp["w_br_b"])[ls], np.asarray(inp["w_br_c"])[ls]], axis=1)),
        "w_o": f32(np.asarray(inp["w_o"])[ls]),
        "w_fg": f32(np.asarray(inp["w_ffn_gate"])[ls]),
        "w_fu": f32(np.asarray(inp["w_ffn_up"])[ls]),
        "w_fd": f32(np.asarray(inp["w_ffn_down"])[ls]),
        "w_pi": f32(np.asarray(inp["w_ple_in"])[ls]),
        "w_pg": f32(np.asarray(inp["w_ple_gate"])[ls]),
        "sm": f32(np.stack([_smalls(inp, l) for l in ls])),
        "cosg": cosg, "sing": sing, "ctab": f32(ctab),
    }
    p = np.asarray(inp["p"], np.float32)
    rpb = np.asarray(inp["rpb"], np.float32)
    nbts = [f32(np.stack([_nbr_tables(rpb[l], half) for l in ls])) for half in range(2)]
    nfin = np.asarray(inp["norm_final"], np.float32).reshape(8, 128).T
    maps = []
    for core in range(8):
        b, half = core // 2, core % 2
        sm2 = np.zeros((128, 16), np.float32)
        sm2[:, 0:8] = nfin
        sm2[:, 8] = 1.0 if half == 1 else 0.0
        sm2[:, 9] = 1.0 if half == 0 else 0.0
        m = dict(shared)
        m["xown"] = f32(xown[core])
        m["xg"] = f32(xg[b])
        m["pT"] = f32(p[ls][:, b, half * 2048:(half + 1) * 2048, :].transpose(0, 2, 1))
        m["sm2"] = sm2
        m["coso"] = f32(cosg[:, half * 2048:(half + 1) * 2048])
        m["sino"] = f32(sing[:, half * 2048:(half + 1) * 2048])
        m["nbt"] = nbts[half]
        maps.append(m)
    return maps


FUSED = True


def kernel(**inp):
    x = np.asarray(inp["x"], np.float32)
    xT = [np.ascontiguousarray(x[c // 2, (c % 2) * 2048:(c % 2 + 1) * 2048, :].T) for c in range(8)]
    xg = [np.stack([xT[2 * b], xT[2 * b + 1]]) for b in range(4)]
    if FUSED:
        nc = _get_nc((0, 1), True, True)
        res = run_bass_kernel_spmd(nc, _in_maps(inp, (0, 1), xT, xg), core_ids=list(range(8)))
        ys = [r["yT"] for r in res.results]
    else:
        nc0 = _get_nc((0,), False, False)
        res = run_bass_kernel_spmd(nc0, _in_maps(inp, (0,), xT, xg), core_ids=list(range(8)))
        xT = [np.asarray(r["yT"], np.float32) for r in res.results]
        xg = [np.stack([xT[2 * b], xT[2 * b + 1]]) for b in range(4)]
        nc1 = _get_nc((1,), True, False)
        res = run_bass_kernel_spmd(nc1, _in_maps(inp, (1,), xT, xg), core_ids=list(range(8)))
        ys = [r["yT"] for r in res.results]
    out = np.empty((4, 4096, 1024), np.float32)
    for c in range(8):
        out[c // 2, (c % 2) * 2048:(c % 2 + 1) * 2048, :] = np.asarray(ys[c], np.float32).T
    return out
```
